# Optimizing a Trainium2 kernel written in Bass

```python
import math
import jax
import jax.numpy as jnp
from jax import lax
import numpy as np

D_MODEL = 1024
BATCH = 2
SEQ = 8192
DEPTH = 2
DEC_BATCH = 4
DEC_SEQ = 8192
PAST_LEN = 128

D_PLE = 256
N_MIX_GROUPS = 4
BRANCH = D_MODEL // 2
D_INNER = N_MIX_GROUPS * BRANCH

RET_HEADS = 4
RET_DK = BRANCH // RET_HEADS
RET_DV = BRANCH // RET_HEADS
HGRN_HEADS = 4
HGRN_DK = BRANCH // HGRN_HEADS
HGRN_DV = BRANCH // HGRN_HEADS
ML_HEADS = 4
ML_DK = BRANCH // ML_HEADS
ML_DV = BRANCH // ML_HEADS
SSD_HEADDIM = 64
SSD_HEADS = BRANCH // SSD_HEADDIM
SSD_GROUPS = 2
SSD_STATE = 128
SSD_CONV = 5
SSD_CONV_DIM = BRANCH + 2 * SSD_GROUPS * SSD_STATE

RET_W = 4 * BRANCH
HGRN_W = 5 * BRANCH
ML_W = 5 * BRANCH + 4 * ML_HEADS
SSD_W = BRANCH + SSD_CONV_DIM + 2 * SSD_HEADS
D_PROJ = RET_W + HGRN_W + ML_W + SSD_W
SPLITS = (RET_W, RET_W + HGRN_W, RET_W + HGRN_W + ML_W)

CHUNK = 64
HGRN_CHUNK = 16
ROPE_BASE = 10000.0
NORM_EPS = 1e-5
NEG = -1e30
DN_ALPHA = (2 * DEPTH) ** 0.25
DN_BETA = (8 * DEPTH) ** -0.25

kernel_name = 'hybrid_bidir_parallel_heads_encoder'


def _layer_norm(x, g, b):
    xf = x.astype(jnp.float32)
    mu = jnp.mean(xf, axis=-1, keepdims=True)
    xc = xf - mu
    var = jnp.mean(xc * xc, axis=-1, keepdims=True)
    return (xc * lax.rsqrt(var + NORM_EPS) * g + b).astype(x.dtype)


def _group_norm(x, g, groups, center):
    b, t, w = x.shape
    xf = x.astype(jnp.float32).reshape(b, t, groups, w // groups)
    if center:
        xf = xf - jnp.mean(xf, axis=-1, keepdims=True)
    y = xf * lax.rsqrt(jnp.mean(xf * xf, axis=-1, keepdims=True) + NORM_EPS)
    return y.reshape(b, t, w) * g


def _rope(x):
    t, d = x.shape[1], x.shape[-1]
    inv = ROPE_BASE ** (-jnp.arange(0, d, 2, dtype=jnp.float32) / d)
    ang = jnp.arange(t, dtype=jnp.float32)[:, None] * inv[None, :]
    cos = jnp.cos(ang)[None, :, None, :]
    sin = jnp.sin(ang)[None, :, None, :]
    xf = x.astype(jnp.float32)
    x1, x2 = xf[..., : d // 2], xf[..., d // 2:]
    return jnp.concatenate([x1 * cos - x2 * sin, x2 * cos + x1 * sin], axis=-1).astype(x.dtype)


def _bidirectional(core, fwd_args, bwd_args):
    fwd = core(*fwd_args)
    bwd = core(*[jnp.flip(a, axis=1) for a in bwd_args])
    return fwd + jnp.flip(bwd, axis=1)


def _linear_recurrence(q, k, v, log_f, chunk):
    b, t, h, dk = q.shape
    dv = v.shape[-1]
    n = t // chunk

    def blocks(a):
        return a.astype(jnp.float32).reshape((b, n, chunk) + a.shape[2:])

    q, k, v, log_f = blocks(q), blocks(k), blocks(v), blocks(log_f)
    cum = jnp.cumsum(log_f, axis=2)
    last = cum[:, :, -1]
    tri = jnp.tril(jnp.ones((chunk, chunk), dtype=bool))
    if log_f.shape[-1] == 1:
        seg = cum[..., 0]
        diff = seg[:, :, :, None, :] - seg[:, :, None, :, :]
        decay = jnp.exp(jnp.where(tri[None, None, :, :, None], diff, NEG))
        scores = jnp.einsum('bnihd,bnjhd->bnijh', q, k) * decay
    else:
        diff = cum[:, :, :, None] - cum[:, :, None, :]
        decay = jnp.exp(jnp.where(tri[None, None, :, :, None, None], diff, NEG))
        scores = jnp.einsum('bnihd,bnjhd,bnijhd->bnijh', q, k, decay)
    intra = jnp.einsum('bnijh,bnjhe->bnihe', scores, v)
    q_in = q * jnp.exp(cum)
    k_in = k * jnp.exp(last[:, :, None] - cum)

    def step(state, xs):
        qc, kc, vc, dc = xs
        out = jnp.einsum('bihd,bhde->bihe', qc, state)
        state = dc[..., None] * state + jnp.einsum('bjhd,bjhe->bhde', kc, vc)
        return state, out

    s0 = jnp.zeros((b, h, dk, dv), jnp.float32)
    xs = tuple(jnp.moveaxis(a, 1, 0) for a in (q_in, k_in, v, jnp.exp(last)))
    _, inter = lax.scan(step, s0, xs)
    return (intra + jnp.moveaxis(inter, 0, 1)).reshape(b, t, h, dv)


def _mlstm_recurrence(q, k, v, i_pre, log_f, chunk):
    b, t, h, dk = q.shape
    dv = v.shape[-1]
    n = t // chunk

    def blocks(a):
        return jnp.moveaxis(a.astype(jnp.float32).reshape((b, n, chunk) + a.shape[2:]), 1, 0)

    qc, kc, vc, ic, fc = blocks(q), blocks(k) * (dk ** -0.5), blocks(v), blocks(i_pre), blocks(log_f)
    causal = jnp.tril(jnp.ones((chunk, chunk), dtype=bool))[None, :, :, None]

    def step(carry, xs):
        c_s, n_s, m = carry
        qx, kx, vx, ix, fx = xs
        bcum = jnp.cumsum(fx, axis=1)
        logw = bcum[:, :, None, :] - bcum[:, None, :, :] + ix[:, None, :, :]
        logw = jnp.where(causal, logw, NEG)
        log_inter = bcum + m[:, None, :]
        m_i = jnp.maximum(log_inter, jnp.max(logw, axis=2))
        w = jnp.exp(logw - m_i[:, :, None, :])
        s_inter = jnp.exp(log_inter - m_i)
        qk = jnp.einsum('bihd,bjhd->bijh', qx, kx) * w
        num = s_inter[..., None] * jnp.einsum('bihd,bhde->bihe', qx, c_s) + jnp.einsum('bijh,bjhe->bihe', qk, vx)
        den = s_inter * jnp.einsum('bihd,bhd->bih', qx, n_s) + jnp.sum(qk, axis=2)
        out = num / jnp.maximum(jnp.abs(den), jnp.exp(-m_i))[..., None]
        b_last = bcum[:, -1]
        log_k = b_last[:, None, :] - bcum + ix
        m_new = jnp.maximum(b_last + m, jnp.max(log_k, axis=1))
        wk = jnp.exp(log_k - m_new[:, None, :])
        sc = jnp.exp(b_last + m - m_new)
        kw = kx * wk[..., None]
        c_new = sc[..., None, None] * c_s + jnp.einsum('bjhd,bjhe->bhde', kw, vx)
        n_new = sc[..., None] * n_s + jnp.sum(kw, axis=1)
        return (c_new, n_new, m_new), out

    init = (jnp.zeros((b, h, dk, dv), jnp.float32), jnp.zeros((b, h, dk), jnp.float32),
            jnp.zeros((b, h), jnp.float32))
    _, out = lax.scan(step, init, (qc, kc, vc, ic, fc))
    return jnp.moveaxis(out, 0, 1).reshape(b, t, h, dv)


def _centred_conv(x, w, bias):
    pad = (SSD_CONV - 1) // 2
    y = lax.conv_general_dilated(x, w[:, None, :].astype(x.dtype), window_strides=(1,),
                                 padding=[(pad, pad)], dimension_numbers=('NWC', 'WIO', 'NWC'),
                                 feature_group_count=x.shape[-1])
    return y + bias.astype(x.dtype)


def _retention(pr, log_rate, norm_g):
    b, t, _ = pr.shape
    q, k, v, g = jnp.split(pr, 4, axis=-1)
    q = _rope(q.reshape(b, t, RET_HEADS, RET_DK))
    k = _rope(k.reshape(b, t, RET_HEADS, RET_DK)) * (RET_DK ** -0.5)
    v = v.reshape(b, t, RET_HEADS, RET_DV)
    lg = -jnp.exp(log_rate.astype(jnp.float32))
    lg_f = jnp.broadcast_to(lg[0][:, None], (b, t, RET_HEADS, 1))
    lg_b = jnp.broadcast_to(lg[1][:, None], (b, t, RET_HEADS, 1))
    o = _bidirectional(lambda *a: _linear_recurrence(*a, CHUNK), (q, k, v, lg_f), (q, k, v, lg_b))
    return _group_norm(o.reshape(b, t, BRANCH), norm_g, RET_HEADS, True) * jax.nn.silu(g)


def _hgrn2(ph, lb, norm_g):
    b, t, _ = ph.shape
    q, zf, zb, i, g = jnp.split(ph, 5, axis=-1)
    shp = (b, t, HGRN_HEADS, HGRN_DK)

    def gates(z, low):
        s = jax.nn.sigmoid(z.astype(jnp.float32))
        log_f = jnp.log(low + (1.0 - low) * s)
        k = (1.0 - low) * (1.0 - s)
        return k.reshape(shp), log_f.reshape(shp)

    k_f, lf_f = gates(zf, lb[0])
    k_b, lf_b = gates(zb, lb[1])
    q = q.reshape(shp)
    v = i.reshape(b, t, HGRN_HEADS, HGRN_DV)
    o = _bidirectional(lambda *a: _linear_recurrence(*a, HGRN_CHUNK), (q, k_f, v, lf_f), (q, k_b, v, lf_b))
    return _group_norm(o.reshape(b, t, BRANCH), norm_g, HGRN_HEADS, False) * jax.nn.silu(g)


def _mlstm(pm, i_bias, f_bias, norm_g):
    b, t, _ = pm.shape
    q, k, v, o, g, gts = jnp.split(pm, [BRANCH, 2 * BRANCH, 3 * BRANCH, 4 * BRANCH, 5 * BRANCH], axis=-1)
    shp = (b, t, ML_HEADS, ML_DK)
    gts = gts.astype(jnp.float32).reshape(b, t, 4, ML_HEADS)
    i_pre = gts[:, :, 0:2] + i_bias
    log_f = jax.nn.log_sigmoid(gts[:, :, 2:4] + f_bias)
    q, k = q.reshape(shp), k.reshape(shp)
    v = v.reshape(b, t, ML_HEADS, ML_DV)
    h = _bidirectional(lambda *a: _mlstm_recurrence(*a, CHUNK),
                       (q, k, v, i_pre[:, :, 0], log_f[:, :, 0]),
                       (q, k, v, i_pre[:, :, 1], log_f[:, :, 1]))
    h = jax.nn.sigmoid(o.astype(jnp.float32)) * h.reshape(b, t, BRANCH)
    return _group_norm(h, norm_g, ML_HEADS, True) * jax.nn.silu(g)


def _ssd(ps, conv_w, conv_b, a_log, dt_bias, d_skip, norm_g):
    b, t, _ = ps.shape
    z, xbc, dt_raw = jnp.split(ps, [BRANCH, BRANCH + SSD_CONV_DIM], axis=-1)
    xbc = jax.nn.silu(_centred_conv(xbc, conv_w, conv_b))
    xs, bm, cm = jnp.split(xbc, [BRANCH, BRANCH + SSD_GROUPS * SSD_STATE], axis=-1)
    rep = SSD_HEADS // SSD_GROUPS
    xs = xs.reshape(b, t, SSD_HEADS, SSD_HEADDIM)
    bm = jnp.repeat(bm.reshape(b, t, SSD_GROUPS, SSD_STATE), rep, axis=2)
    cm = jnp.repeat(cm.reshape(b, t, SSD_GROUPS, SSD_STATE), rep, axis=2)
    dt = jax.nn.softplus(dt_raw.astype(jnp.float32).reshape(b, t, 2, SSD_HEADS) + dt_bias)
    a = -jnp.exp(a_log.astype(jnp.float32))
    log_decay = dt * a
    y = _bidirectional(lambda *args: _linear_recurrence(*args, CHUNK),
                       (cm, bm * dt[:, :, 0, :, None], xs, log_decay[:, :, 0, :, None]),
                       (cm, bm * dt[:, :, 1, :, None], xs, log_decay[:, :, 1, :, None]))
    y = y + d_skip[:, None] * xs
    return _group_norm(y.reshape(b, t, BRANCH) * jax.nn.silu(z), norm_g, SSD_GROUPS, False)


def _layer(x, p, w_in, w_out, ln_g, ln_b, ret_log_rate, ret_norm_g, lb, hgrn_norm_g,
           ml_i_bias, ml_f_bias, ml_norm_g, ssd_conv_w, ssd_conv_b, ssd_a_log, ssd_dt_bias,
           ssd_d, ssd_norm_g, w_ple_gate, w_ple_proj):
    proj = jnp.einsum('btd,de->bte', x, w_in)
    pr, ph, pm, ps = jnp.split(proj, SPLITS, axis=-1)
    mixed = jnp.concatenate([
        _retention(pr, ret_log_rate, ret_norm_g),
        _hgrn2(ph, lb, hgrn_norm_g),
        _mlstm(pm, ml_i_bias, ml_f_bias, ml_norm_g),
        _ssd(ps, ssd_conv_w, ssd_conv_b, ssd_a_log, ssd_dt_bias, ssd_d, ssd_norm_g),
    ], axis=-1).astype(x.dtype)
    y = jnp.einsum('bte,ed->btd', mixed, w_out)
    x = _layer_norm(DN_ALPHA * x + y, ln_g, ln_b)
    gate = jax.nn.sigmoid(jnp.einsum('btd,de->bte', x, w_ple_gate).astype(jnp.float32))
    return (x + gate * jnp.einsum('btp,pd->btd', p, w_ple_proj)).astype(x.dtype)


def _trunk(x, p, params):
    (w_in, w_out, ln_g, ln_b, ret_log_rate, ret_norm_g, hgrn_lb_logits, hgrn_norm_g,
     mlstm_i_bias, mlstm_f_bias, mlstm_norm_g, ssd_conv_w, ssd_conv_b, ssd_a_log,
     ssd_dt_bias, ssd_d, ssd_norm_g, w_ple_gate, w_ple_proj) = params
    lb_w = jax.nn.softmax(hgrn_lb_logits.astype(jnp.float32), axis=0)
    lb = jnp.cumsum(lb_w, axis=0) - lb_w[0:1]
    for l in range(DEPTH):
        x = _layer(x, p[l], w_in[l], w_out[l], ln_g[l], ln_b[l], ret_log_rate[l], ret_norm_g[l],
                   lb[l], hgrn_norm_g[l], mlstm_i_bias[l], mlstm_f_bias[l], mlstm_norm_g[l],
                   ssd_conv_w[l], ssd_conv_b[l], ssd_a_log[l], ssd_dt_bias[l], ssd_d[l],
                   ssd_norm_g[l], w_ple_gate[l], w_ple_proj[l])
    return x


def setup_inputs(seed: int = 0) -> dict:
    key = jax.random.key(seed)
    ks = jax.random.split(key, 24)
    f32 = jnp.float32

    def nrm(k, shape, scale):
        return scale * jax.random.normal(k, shape, f32)

    x_prompt = nrm(ks[0], (BATCH, SEQ, D_MODEL), 1.0)
    x_sample = nrm(ks[1], (DEC_BATCH, DEC_SEQ, D_MODEL), 1.0)
    p_prompt = nrm(ks[2], (DEPTH, BATCH, SEQ, D_PLE), 1.0)
    p_sample = nrm(ks[3], (DEPTH, DEC_BATCH, DEC_SEQ, D_PLE), 1.0)
    w_in = nrm(ks[4], (DEPTH, D_MODEL, D_PROJ), D_MODEL ** -0.5)
    w_out = nrm(ks[5], (DEPTH, D_INNER, D_MODEL), DN_BETA * D_INNER ** -0.5)
    ln_g = 1.0 + nrm(ks[6], (DEPTH, D_MODEL), 0.02)
    ln_b = nrm(ks[7], (DEPTH, D_MODEL), 0.02)
    ret_exp = jnp.linspace(5.0, 12.0, RET_HEADS, dtype=f32)
    ret_log_rate = jnp.log(-jnp.log1p(-jnp.exp2(-ret_exp))) + nrm(ks[8], (DEPTH, 2, RET_HEADS), 0.05)
    ret_norm_g = 1.0 + nrm(ks[9], (DEPTH, BRANCH), 0.02)
    hgrn_lb_logits = nrm(ks[10], (DEPTH, 2, BRANCH), 0.1)
    hgrn_norm_g = 1.0 + nrm(ks[11], (DEPTH, BRANCH), 0.02)
    mlstm_i_bias = nrm(ks[12], (DEPTH, 2, ML_HEADS), 0.1)
    mlstm_f_bias = jnp.linspace(3.0, 6.0, ML_HEADS, dtype=f32) + nrm(ks[13], (DEPTH, 2, ML_HEADS), 0.1)
    mlstm_norm_g = 1.0 + nrm(ks[14], (DEPTH, BRANCH), 0.02)
    ssd_conv_w = nrm(ks[15], (DEPTH, SSD_CONV, SSD_CONV_DIM), SSD_CONV ** -0.5)
    ssd_conv_b = nrm(ks[16], (DEPTH, SSD_CONV_DIM), 0.02)
    ssd_a_log = jnp.log(jax.random.uniform(ks[17], (DEPTH, 2, SSD_HEADS), f32, 1.0, 16.0))
    dt0 = jnp.exp(jax.random.uniform(ks[18], (DEPTH, 2, SSD_HEADS), f32, math.log(1e-3), math.log(1e-1)))
    ssd_dt_bias = dt0 + jnp.log(-jnp.expm1(-dt0))
    ssd_d = 1.0 + nrm(ks[19], (DEPTH, SSD_HEADS), 0.02)
    ssd_norm_g = 1.0 + nrm(ks[20], (DEPTH, BRANCH), 0.02)
    w_ple_gate = nrm(ks[21], (DEPTH, D_MODEL, D_MODEL), D_MODEL ** -0.5)
    w_ple_proj = nrm(ks[22], (DEPTH, D_PLE, D_MODEL), D_PLE ** -0.5)
    return {'x_prompt': x_prompt, 'x_sample': x_sample, 'p_prompt': p_prompt, 'p_sample': p_sample,
            'w_in': w_in, 'w_out': w_out, 'ln_g': ln_g, 'ln_b': ln_b,
            'ret_log_rate': ret_log_rate, 'ret_norm_g': ret_norm_g,
            'hgrn_lb_logits': hgrn_lb_logits, 'hgrn_norm_g': hgrn_norm_g,
            'mlstm_i_bias': mlstm_i_bias, 'mlstm_f_bias': mlstm_f_bias, 'mlstm_norm_g': mlstm_norm_g,
            'ssd_conv_w': ssd_conv_w, 'ssd_conv_b': ssd_conv_b, 'ssd_a_log': ssd_a_log,
            'ssd_dt_bias': ssd_dt_bias, 'ssd_d': ssd_d, 'ssd_norm_g': ssd_norm_g,
            'w_ple_gate': w_ple_gate, 'w_ple_proj': w_ple_proj}


def reference(x_prompt, x_sample, p_prompt, p_sample, w_in, w_out, ln_g, ln_b,
              ret_log_rate, ret_norm_g, hgrn_lb_logits, hgrn_norm_g,
              mlstm_i_bias, mlstm_f_bias, mlstm_norm_g,
              ssd_conv_w, ssd_conv_b, ssd_a_log, ssd_dt_bias, ssd_d, ssd_norm_g,
              w_ple_gate, w_ple_proj):
    params = (w_in, w_out, ln_g, ln_b, ret_log_rate, ret_norm_g, hgrn_lb_logits, hgrn_norm_g,
              mlstm_i_bias, mlstm_f_bias, mlstm_norm_g, ssd_conv_w, ssd_conv_b, ssd_a_log,
              ssd_dt_bias, ssd_d, ssd_norm_g, w_ple_gate, w_ple_proj)
    y_prompt = _trunk(x_prompt, p_prompt, params)
    y_sample = _trunk(x_sample, p_sample, params)
    return (y_prompt, y_sample)
```

```python
import math
from contextlib import ExitStack

import numpy as np

import concourse.bass as bass
import concourse.mybir as mybir
from concourse.bass_utils import run_bass_kernel_spmd

F32 = mybir.dt.float32
BF16 = mybir.dt.bfloat16
AF = mybir.ActivationFunctionType
ALU = mybir.AluOpType
AX = mybir.AxisListType

C = 128
NSUB = 4
TB = 512
DM = 1024
DP = 8736
DEPTH = 2
DPLE = 256
RET0, HG0, ML0, SS0 = 0, 2048, 4608, 7184
EPS = 1e-5
DN_ALPHA = (2 * DEPTH) ** 0.25
NSTATE = 2576
ST_RET, ST_HG0, ST_HG1, ST_ML, ST_SSD = 0, 512, 1024, 1536, 2064
NPIECE = 23


def _dtsize(dt):
    if dt == F32:
        return 4
    if dt == BF16:
        return 2
    return 4


class Sched:
    NRING = 8

    def __init__(self, nc, stack):
        self.nc = nc
        self.eng = {'pe': nc.tensor, 'dve': nc.vector, 'act': nc.scalar, 'pool': nc.gpsimd, 'sp': nc.sync}
        self.prog, self.cnt, self.seen, self.snaps = {}, {}, {}, {}
        for e in self.eng:
            self.prog[e] = stack.enter_context(nc.semaphore('pg_' + e))
            self.cnt[e] = 0
            self.seen[e] = {}
            self.snaps[('c', e)] = {}
        self.dsem, self.dcnt = {}, {}
        for q in ('sp', 'act', 'pool'):
            self.dsem[q] = [stack.enter_context(nc.semaphore('dq_%s_%d' % (q, i))) for i in range(self.NRING)]
            self.dcnt[q] = 0
            for i in range(self.NRING):
                self.snaps[('d', q, i)] = {}
        self.lastw, self.reads = {}, {}
        self.partbytes = {}
        self.psum_names = set()
        self.nwaits = 0
        self.nops = 0

    def keys(self, ap):
        if isinstance(ap, (tuple, str)):
            return [ap]
        name = ap.name
        ps = self.partbytes.get(name)
        if ps is None:
            return [name]
        esz = _dtsize(ap.dtype)
        a = ap.ap
        pst = a[0][0]
        off = ap.offset % pst if pst > 0 else ap.offset
        ext = 1
        for st, cn in a[1:]:
            ext += abs(st) * (cn - 1)
        lo = off * esz
        hi = (off + ext) * esz
        return [(name, i) for i in range(lo // ps, (hi - 1) // ps + 1)]

    def _sem(self, semkey):
        if semkey[0] == 'c':
            return self.prog[semkey[1]]
        return self.dsem[semkey[1]][semkey[2]]

    def _need(self, e, ev, out):
        semkey, val = ev
        if semkey[0] == 'c' and semkey[1] == e and e == 'pe':
            return
        if self.seen[e].get(semkey, 0) >= val:
            return
        if out.get(semkey, 0) < val:
            out[semkey] = val

    def _deps(self, e, rk, wk):
        need = {}
        for r in rk:
            ev = self.lastw.get(r)
            if ev is not None:
                self._need(e, ev, need)
        for w in wk:
            ev = self.lastw.get(w)
            if ev is not None:
                self._need(e, ev, need)
            for ev in self.reads.get(w, ()):
                self._need(e, ev, need)
        return need

    def _emit_waits(self, e, need):
        for semkey, val in need.items():
            if self.seen[e].get(semkey, 0) >= val:
                continue
            self.eng[e].wait_ge(self._sem(semkey), val)
            self.nwaits += 1
            new = dict(self.seen[e])
            new[semkey] = val
            snap = self.snaps[semkey].get(val)
            if snap:
                for k, v in snap.items():
                    if new.get(k, 0) < v:
                        new[k] = v
            self.seen[e] = new

    def _record(self, ev, rk, wk):
        for r in rk:
            lst = self.reads.setdefault(r, [])
            lst[:] = [x for x in lst if x[0] != ev[0]]
            lst.append(ev)
        for w in wk:
            self.lastw[w] = ev
            self.reads[w] = []

    def _is_psum(self, ap):
        return (not isinstance(ap, (tuple, str))) and ap.name in self.psum_names

    def _rw(self, reads, writes):
        rk, wk, rrec, wrec = [], [], [], []
        for r in reads:
            if r is None or isinstance(r, (int, float)):
                continue
            k = self.keys(r)
            if self._is_psum(r):
                rk.append(('bank', r.name))
                rrec += k
            else:
                rk += k
                rrec += k
        for w in writes:
            k = self.keys(w)
            wk += k
            wrec += k
            if self._is_psum(w):
                wrec.append(('bank', w.name))
        return rk, wk, rrec, wrec

    def op(self, e, fn, reads=(), writes=()):
        rk, wk, rrec, wrec = self._rw(reads, writes)
        self._emit_waits(e, self._deps(e, rk, wk))
        ins = fn(self.eng[e])
        self.cnt[e] += 1
        ins.then_inc(self.prog[e], 1)
        ev = (('c', e), self.cnt[e])
        self.snaps[('c', e)][self.cnt[e]] = self.seen[e]
        self._record(ev, rrec, wrec)
        self.nops += 1
        return ins

    def dma(self, q, out, in_, reads=None, writes=None, **kw):
        rk, wk, rrec, wrec = self._rw([in_] if reads is None else reads, [out] if writes is None else writes)
        n = self.dcnt[q]
        k = n % self.NRING
        prev = 16 * (n // self.NRING)
        need = self._deps(q, rk, wk)
        semkey = ('d', q, k)
        if prev > 0 and self.seen[q].get(semkey, 0) < prev:
            need[semkey] = max(need.get(semkey, 0), prev)
        self._emit_waits(q, need)
        ins = self.eng[q].dma_start(out=out, in_=in_, **kw)
        val = prev + 16
        ins.then_inc(self.dsem[q][k], 16)
        self.dcnt[q] = n + 1
        ev = (semkey, val)
        self.snaps[semkey][val] = self.seen[q]
        self._record(ev, rrec, wrec)
        self.nops += 1
        return ev

    def finish(self):
        for q in ('sp', 'act', 'pool'):
            n = self.dcnt[q]
            for k in range(self.NRING):
                cnt_k = (n - k + self.NRING - 1) // self.NRING if n > k else 0
                if cnt_k > 0:
                    self.eng[q].wait_ge(self.dsem[q][k], 16 * cnt_k)

    def mm(self, out, lhsT, rhs, start=True, stop=True):
        return self.op('pe', lambda e: e.matmul(out, lhsT, rhs, start=start, stop=stop),
                       reads=[lhsT, rhs], writes=[out])

    def tr(self, out, in_, ident):
        return self.op('pe', lambda e: e.transpose(out, in_, ident), reads=[in_, ident], writes=[out])

    def act(self, out, in_, func, bias=None, scale=None, accum=None, eng='act'):
        kw = {}
        if bias is not None:
            kw['bias'] = bias
        if scale is not None:
            kw['scale'] = scale
        if accum is not None:
            kw['accum_out'] = accum
        w = [out] + ([accum] if accum is not None else [])
        return self.op(eng, lambda e: e.activation(out, in_, func, **kw), reads=[in_, bias, scale], writes=w)

    def tt(self, eng, out, a, b, op):
        return self.op(eng, lambda e: e.tensor_tensor(out, a, b, op), reads=[a, b], writes=[out])

    def ts(self, eng, out, a, s1, s2, op0, op1=None):
        if op1 is None:
            return self.op(eng, lambda e: e.tensor_scalar(out, a, s1, None, op0), reads=[a, s1], writes=[out])
        return self.op(eng, lambda e: e.tensor_scalar(out, a, s1, s2, op0, op1), reads=[a, s1, s2], writes=[out])

    def stt(self, eng, out, a, s, b, op0, op1):
        return self.op(eng, lambda e: e.scalar_tensor_tensor(out, a, s, b, op0, op1), reads=[a, s, b], writes=[out])

    def cp(self, eng, out, in_):
        if eng == 'act':
            return self.op('act', lambda e: e.copy(out, in_), reads=[in_], writes=[out])
        return self.op(eng, lambda e: e.tensor_copy(out, in_), reads=[in_], writes=[out])

    def memset(self, eng, out, val):
        return self.op(eng, lambda e: e.memset(out, val), writes=[out])

    def rsum(self, eng, out, in_):
        return self.op(eng, lambda e: e.reduce_sum(out, in_, AX.X), reads=[in_], writes=[out])

    def recip(self, out, in_):
        return self.op('dve', lambda e: e.reciprocal(out, in_), reads=[in_], writes=[out])

    def scan(self, out, d0, d1, init, op0, op1):
        return self.op('dve', lambda e: e.tensor_tensor_scan(out, d0, d1, init, op0, op1), reads=[d0, d1], writes=[out])


CST = {}


def _cst_layout():
    off = 0
    for name, w in (('IDENT', 128), ('LOW', 128), ('UP', 128), ('ONES', 128), ('DIFF', 128),
                    ('NEGF', 128), ('NEGB', 128), ('COLS', 8), ('LOW64', 64), ('UP64', 64), ('ROWF', 128), ('ROWB', 128)):
        CST[name] = (off, w)
        off += w
    return off


NCST = _cst_layout()


def host_consts(T):
    p = np.arange(128)[:, None].astype(np.float64)
    i = np.arange(128)[None, :].astype(np.float64)
    c = np.zeros((128, NCST), np.float32)

    def put(name, arr):
        o, w = CST[name]
        c[:, o:o + w] = arr

    put('IDENT', (p == i))
    put('LOW', (i >= p))
    put('UP', (i <= p))
    put('ONES', np.ones((128, 128)))
    put('DIFF', i - p)
    put('NEGF', np.where(i >= p, 0.0, -1e30))
    put('NEGB', np.where(i <= p, 0.0, -1e30))
    cols = np.zeros((128, 8))
    j = np.arange(128)
    cols[:, 0] = j + 1
    cols[:, 1] = C - j
    cols[:, 2] = C - 1 - j
    cols[:, 3] = j
    cols[:, 4] = EPS
    put('COLS', cols)
    i64 = np.arange(64)[None, :]
    p64 = (np.arange(128) % 64)[:, None]
    put('LOW64', (i64 >= p64))
    put('UP64', (i64 <= p64))
    put('ROWF', np.broadcast_to(i + 1, (128, 128)))
    put('ROWB', np.broadcast_to(C - i, (128, 128)))
    d = 128
    inv = 10000.0 ** (-np.arange(0, d, 2, dtype=np.float32) / d)
    ang = np.arange(T, dtype=np.float32)[:, None] * inv[None, :]
    cos = np.cos(ang).astype(np.float32)
    sin = np.sin(ang).astype(np.float32)
    s = np.float32(128 ** -0.5)
    rope = np.concatenate([cos, sin, cos * s, sin * s], axis=1).astype(np.float32)
    return c, rope


class Prog:
    def __init__(self, T, debug=None, mixers=('ret', 'hg', 'ml', 'ssd'), nlayers=DEPTH):
        self.T = T
        self.NBLK = T // TB
        self.NCH = T // C
        self.debug = debug
        self.mixers = mixers
        self.nlayers = nlayers
        self.nc = bass.Bass("TRN2", target_bir_lowering=False)
        self.stack = ExitStack()
        self.dbg_outs = {}

    def sb(self, name, shape, dt=F32, part=None):
        t = self.stack.enter_context(self.nc.sbuf_tensor(name, shape, dt))
        if part is not None:
            self.S.partbytes[name] = part
        return t

    def ps(self, name, shape, dt=F32, part=None):
        t = self.stack.enter_context(self.nc.psum_tensor(name, shape, dt))
        if part is not None:
            self.S.partbytes[name] = part
        return t

    def din(self, name, shape, dt=F32):
        return self.nc.dram_tensor(name, list(shape), dt, kind="ExternalInput").ap()

    def dout(self, name, shape, dt=F32):
        return self.nc.dram_tensor(name, list(shape), dt, kind="ExternalOutput").ap()

    def dscr(self, name, shape, dt=F32):
        return self.nc.dram_tensor(name, list(shape), dt).ap()

    def cs(self, name, lo=0, hi=None):
        o, w = CST[name]
        if hi is None:
            hi = w
        return self.cst[:, o + lo:o + hi]

    def build(self):
        nc = self.nc
        T = self.T
        st = self.stack
        self.S = S = Sched(nc, st)
        self.x_d = self.din("x", [T, DM])
        self.p_d = self.din("pp", [DEPTH, T, DPLE])
        self.w_in_d = self.din("w_in", [DEPTH, DM, DP])
        self.w_out_d = self.din("w_out", [DEPTH, 2048, DM])
        self.w_gate_d = self.din("w_ple_gate", [DEPTH, DM, DM])
        self.w_pp_d = self.din("w_ple_proj", [DEPTH, DPLE, DM])
        self.ln_g_d = self.din("ln_g", [DEPTH, DM])
        self.ln_b_d = self.din("ln_b", [DEPTH, DM])
        self.ret_lr_d = self.din("ret_log_rate", [DEPTH, 8])
        self.gn_g_d = self.din("gn_g", [DEPTH, 2048])
        self.hg_lb_d = self.din("hgrn_lb_logits", [DEPTH, 2, 512])
        self.ml_b_d = self.din("ml_bias", [DEPTH, 16])
        self.conv_w_d = self.din("ssd_conv_w", [DEPTH, 5, 1024])
        self.conv_b_d = self.din("ssd_conv_b", [DEPTH, 1024])
        self.a_log_d = self.din("ssd_a_log", [DEPTH, 16])
        self.dt_b_d = self.din("ssd_dt_bias", [DEPTH, 16])
        self.ssd_d_d = self.din("ssd_d", [DEPTH, 8])
        self.cst_d = self.din("cst", [128, NCST])
        self.rope_d = self.din("rope", [T, 256])
        self.y_d = self.dout("y", [T, DM])
        self.h1_d = self.dscr("h1", [T, DM])
        self.wbf_d = self.dscr("wbf", [DEPTH, NPIECE, 128, 4096], BF16)
        self.stsc_d = self.dscr("stsc", [self.NCH, 128, NSTATE], BF16)

        self.P = [self.ps("P%d" % i, [128, 512], F32) for i in range(8)]
        self.pj_i = 0
        self.pj4_i = 0
        self.pj3_i = 0

        self.cst = self.sb("cst_sb", [128, NCST])
        self.identb = self.sb("identb", [128, 128], BF16)
        self.negfb = self.sb("negfb", [128, 128], BF16)
        self.negbb = self.sb("negbb", [128, 128], BF16)
        S.dma('sp', self.cst[:], self.cst_d)
        S.cp('dve', self.identb[:], self.cs('IDENT'))
        S.cp('dve', self.negfb[:], self.cs('NEGF'))
        S.cp('dve', self.negbb[:], self.cs('NEGB'))

        self.NW = 4
        self.WHOLD = 2
        self.wring = [self.sb("wr%d" % i, [128, 8, 512], BF16) for i in range(self.NW)]
        self.wsm = self.sb("wsm", [128, 8, 32], BF16)
        self.wpp = self.sb("wpp", [128, 2, 1024], BF16)
        self.wplan = []
        self.wplan_pos = 0
        self.wissued = 0

        self.g32 = [self.sb("g32_%d" % i, [128, 4, 512], F32, part=2048) for i in range(6)]
        big16 = self.sb("g16_01", [128, 4128], BF16, part=1024)
        self.g16 = [big16[:, 0:2048].rearrange("p (a c) -> p a c", a=4), big16[:, 2048:4096].rearrange("p (a c) -> p a c", a=4)]
        self.g16 += [self.sb("g16_%d" % i, [128, 4, 512], BF16, part=1024) for i in range(2, 7)]
        self.pre16 = big16[:, 0:4128].rearrange("p (t c) -> p t c", t=8)
        self.xT = self.sb("xT", [128, 8, 512], BF16, part=1024)
        self.mixed = self.sb("mixed", [128, 4, 2048], BF16, part=1024)
        self.gn_g = self.sb("gn_g_t", [128, 2048])
        self.ln_g = self.sb("ln_g_t", [128, DM])
        self.ln_b = self.sb("ln_b_t", [128, DM])
        self.small = self.sb("small", [128, 512])
        self.mc = self.sb("mc", [128, 4, 128])
        self.efb = self.sb("efb", [128, 2, 4, 128])
        self.Sf_ret = self.sb("S_ret", [128, 4, 128])
        self.Sb_ret = self.Sf_ret
        self.sh_ret = [self.sb("sh_ret%d" % i, [128, 512], BF16) for i in range(2)]
        self.Sf_ret16 = self.sh_ret[0][:].rearrange("p (h e) -> p h e", h=4)
        self.sbl = [self.sb("sbl%d" % i, [128, NSTATE], BF16, part=1024) for i in range(2)]
        self.tiny = self.sb("tiny", [128, 256])
        self.sm2 = self.sb("sm2", [128, 512])
        self.dg = self.sb("dg", [128, 5, 8, 128], BF16)
        self.S_ssd = self.sb("S_ssd", [128, 512])
        self.sh_ssd = [self.sb("sh_ssd%d" % i, [128, 512], BF16) for i in range(2)]
        self.xh32 = self.g32[5][0:4, 2:4, :].rearrange("p a c -> p (a c)")
        self.S_ml = self.sb("S_ml", [128, 4, 132])
        self.sh_ml = [self.sb("sh_ml%d" % i, [128, 528], BF16) for i in range(2)]
        self.S_hg = self.sb("S_hg", [128, 4, 128])
        self.sh_hg = [self.sb("sh_hg%d" % i, [128, 512], BF16) for i in range(2)]
        self.hgdec = self.sb("hgdec", [128, 64])
        self.hglb = self.sb("hglb", [128, 32])
        self.ones16 = self.sb("ones16", [128, 8], BF16)
        S.memset('dve', self.ones16[:], 1.0)
        for i in range(2):
            S.memset('dve', self.sh_ml[i][:], 0.0)
        self.hg_lb_setup()
        self.xTh = self.sb("xTh", [128, 8, 4], BF16)
        self.dbg_t = None

        if len(self.mixers) < 4:
            S.memset('pool', self.mixed[:], 0.0)
        self.convert_weights()
        for l in range(self.nlayers):
            self.layer_params(l)
            if 'ssd' in self.mixers:
                self.layer_params_ssd(l)
            self.plan_layer(l)
            src = self.x_d if l == 0 else self.h1_d
            dst = self.y_d if l == self.nlayers - 1 else self.h1_d
            self.zero_states_bwd()
            for blk in reversed(range(self.NBLK)):
                self.pre_block(l, blk, src)
            self.zero_states_fwd()
            for blk in range(self.NBLK):
                self.main_block(l, blk, src, dst)
        S.finish()
        print('sbuf bytes remaining', nc.sbuf_bytes_remaining, 'ops', S.nops, 'waits', S.nwaits)
        st.close()
        return nc

    def piece_src(self, l, pid):
        if pid < 17:
            cols = [RET0, RET0 + 512, RET0 + 1024, RET0 + 1536,
                    HG0, HG0 + 512, HG0 + 1024, HG0 + 1536, HG0 + 2048,
                    ML0, ML0 + 512, ML0 + 1024, ML0 + 1536, ML0 + 2048,
                    SS0, SS0 + 512, SS0 + 1024][pid]
            return self.w_in_d[l].rearrange("(k p) c -> p k c", p=128)[:, :, cols:cols + 512]
        if pid < 21:
            q = pid - 17
            kh, ch = q % 2, q // 2
            return self.w_out_d[l][kh * 1024:(kh + 1) * 1024, :].rearrange("(k p) c -> p k c", p=128)[:, :, ch * 512:(ch + 1) * 512]
        ch = pid - 21
        return self.w_gate_d[l].rearrange("(k p) c -> p k c", p=128)[:, :, ch * 512:(ch + 1) * 512]

    def convert_weights(self):
        S = self.S
        engs = ['dve', 'act', 'pool']
        n = 0
        for l in range(self.nlayers):
            for pid in range(NPIECE):
                stg = self.g32[2 * (n % 2)], self.g32[2 * (n % 2) + 1]
                src = self.piece_src(l, pid)
                S.dma('sp', stg[0][:], src[:, 0:4, :])
                S.dma('sp', stg[1][:], src[:, 4:8, :])
                ob = self.wring[n % self.NW]
                e = engs[n % 3]
                S.cp(e, ob[:, 0:4, :], stg[0][:])
                S.cp(engs[(n + 1) % 3], ob[:, 4:8, :], stg[1][:])
                S.dma('pool', self.wbf_d[l, pid].rearrange("p (k c) -> p k c", k=8), ob[:],
                      writes=[('wbf', l, pid)])
                n += 1

    def plan_layer(self, l):
        pre, main = [], []
        if 'ret' in self.mixers:
            pre += [1, 2]
            main += [0, 1, 2, 3]
        if 'hg' in self.mixers:
            pre += [6, 7]
            main += [4, 5, 6, 7, 8]
        if 'ml' in self.mixers:
            pre += [10, 11]
            main += [9, 10, 11, 12, 13]
        if 'ssd' in self.mixers:
            pre += [15, 16]
            main += [14, 15, 16]
        tail = [17, 18, 19, 20, 21, 22]
        for blk in range(self.NBLK):
            for pid in pre:
                self.wplan.append((l, pid))
        for blk in range(self.NBLK):
            for pid in main + tail:
                self.wplan.append((l, pid))

    def _issue_piece(self):
        if self.wissued >= len(self.wplan):
            return
        l, pid = self.wplan[self.wissued]
        slot = self.wring[self.wissued % self.NW]
        self.S.dma('sp', slot[:], self.wbf_d[l, pid].rearrange("p (k c) -> p k c", k=8),
                   reads=[('wbf', l, pid)])
        self.wissued += 1

    def next_piece(self, l, pid):
        assert self.wplan[self.wplan_pos] == (l, pid), (self.wplan[self.wplan_pos], l, pid)
        while self.wissued < min(len(self.wplan), max(self.wplan_pos + 1, self.wplan_pos - self.WHOLD + 1 + self.NW)):
            self._issue_piece()
        slot = self.wring[self.wplan_pos % self.NW]
        self.wplan_pos += 1
        return slot

    def layer_params(self, l):
        S = self.S
        S.dma('pool', self.gn_g[:], self.gn_g_d[l].partition_broadcast(128))
        S.dma('pool', self.ln_g[:], self.ln_g_d[l].partition_broadcast(128))
        S.dma('pool', self.ln_b[:], self.ln_b_d[l].partition_broadcast(128))
        wv = self.w_in_d[l].rearrange("(k p) c -> p k c", p=128)
        v = self.g32[2][:, 0, 0:256].rearrange("p (k c) -> p k c", k=8)
        S.dma('pool', v[:, :, 0:16], wv[:, :, ML0 + 2560:ML0 + 2576])
        S.dma('pool', v[:, :, 16:32], wv[:, :, SS0 + 1536:SS0 + 1552])
        S.cp('dve', self.wsm[:], v)
        v2 = self.g32[3][:].rearrange("p a c -> p (a c)").rearrange("p (k c) -> p k c", k=2)
        S.dma('pool', v2, self.w_pp_d[l].rearrange("(k p) c -> p k c", p=128))
        S.cp('dve', self.wpp[:], v2)
        S.dma('pool', self.small[:, 160:176], self.ml_b_d[l].partition_broadcast(128))
        sm = self.small
        S.dma('pool', sm[:, 0:8], self.ret_lr_d[l].partition_broadcast(128))
        S.act(sm[:, 16:24], sm[:, 0:8], AF.Exp)
        S.ts('dve', sm[:, 8:16], sm[:, 16:24], -1.0, None, ALU.mult)
        cols = self.cs('COLS')
        for h in range(4):
            lgf = sm[:, 8 + h:9 + h]
            lgb = sm[:, 12 + h:13 + h]
            S.act(sm[:, 32 + h:33 + h], cols[:, 0:1], AF.Exp, scale=lgf)
            S.act(sm[:, 36 + h:37 + h], cols[:, 1:2], AF.Exp, scale=lgb)
            S.act(sm[:, 40 + h:41 + h], cols[:, 2:3], AF.Exp, scale=lgf)
            S.act(sm[:, 44 + h:45 + h], cols[:, 3:4], AF.Exp, scale=lgb)
            S.act(self.efb[:, 0, h, :], self.cs('ROWF'), AF.Exp, scale=lgf)
            S.act(self.efb[:, 1, h, :], self.cs('ROWB'), AF.Exp, scale=lgb)
            S.act(sm[:, 48 + h:49 + h], lgf, AF.Exp, scale=float(C))
            S.act(sm[:, 52 + h:53 + h], lgb, AF.Exp, scale=float(C))
            t1 = self.g32[0][:, 0, 0:128]
            t2 = self.g32[0][:, 0, 128:256]
            S.act(t1, self.cs('DIFF'), AF.Exp, scale=lgf)
            S.tt('dve', t1, t1, self.cs('LOW'), ALU.mult)
            S.act(t2, self.cs('DIFF'), AF.Exp, scale=sm[:, 20 + h:21 + h])
            S.tt('dve', t2, t2, self.cs('UP'), ALU.mult)
            S.tt('dve', self.mc[:, h, :], t1, t2, ALU.add)

    def layer_params_ssd(self, l):
        S = self.S
        sm = self.small
        S.dma('pool', sm[:, 64:80], self.a_log_d[l].partition_broadcast(128))
        S.act(sm[:, 64:80], sm[:, 64:80], AF.Exp)
        S.ts('dve', sm[:, 64:80], sm[:, 64:80], -1.0, None, ALU.mult)
        S.dma('pool', sm[:, 80:96], self.dt_b_d[l].partition_broadcast(128))
        S.dma('pool', sm[:, 96:104], self.ssd_d_d[l].partition_broadcast(128))
        S.dma('pool', sm[:, 104:112], self.conv_b_d[l].rearrange("(t p) -> p t", p=128), allow_slow_non_contiguous=True)
        cw = sm[:, 112:152].rearrange("p (k t) -> p k t", k=5)
        for k in range(5):
            S.dma('pool', cw[:, k, :], self.conv_w_d[l][k].rearrange("(t p) -> p t", p=128), allow_slow_non_contiguous=True)
        for k in range(5):
            for t in range(8):
                S.ts('dve' if (k + t) % 2 == 0 else 'pool', self.dg[:, k, t, :], self.cs('IDENT'), cw[:, k, t:t + 1], None, ALU.mult)

    def zero_states_bwd(self):
        self.S.memset('dve', self.Sb_ret[:], 0.0)
        self.S.memset('dve', self.S_ssd[:], 0.0)
        self.S.memset('dve', self.S_ml[:], 0.0)
        self.S.memset('dve', self.S_hg[:], 0.0)

    def zero_states_fwd(self):
        self.S.memset('dve', self.Sf_ret[:], 0.0)
        self.S.memset('dve', self.Sf_ret16, 0.0)
        self.S.memset('dve', self.S_ssd[:], 0.0)
        self.S.memset('dve', self.sh_ssd[0][:], 0.0)
        self.S.memset('dve', self.S_ml[:], 0.0)
        self.S.memset('dve', self.sh_ml[0][:], 0.0)
        self.S.memset('dve', self.S_hg[:], 0.0)
        self.S.memset('dve', self.sh_hg[0][:], 0.0)

    def xs32(self, s):
        return self.g32[2 + s // 2][:, 2 * (s % 2):2 * (s % 2) + 2, :].rearrange("p a c -> p (a c)")

    def load_x32(self, blk, src, l):
        for hf in range(2):
            self.S.dma('pool', self.g32[2 + hf][:].rearrange("p (s a) c -> p s (a c)", s=2),
                       src[blk * TB + hf * 256:blk * TB + (hf + 1) * 256, :].rearrange("(s p) d -> p s d", p=128),
                       reads=[('hx', l, blk * NSUB + 2 * hf + s) for s in range(2)])

    def load_xT(self, blk, src):
        S = self.S
        if getattr(self, 'x_prefetched', None) == (self.cur_layer, blk, self.in_main):
            self.x_prefetched = None
        else:
            self.load_x32(blk, src, self.cur_layer)
        ident = self.cs('IDENT')
        for k in range(8):
            bank = self.P[2 + (k % 2)]
            for s in range(NSUB):
                S.tr(bank[:, s * 128:(s + 1) * 128], self.xs32(s)[:, k * 128:(k + 1) * 128], ident)
            S.cp('act' if k % 2 == 0 else 'dve', self.xT[:, k, :], bank[:, :])

    def load_rope(self, blk):
        t = self.g32[5][:, :, 0:256]
        self.S.dma('pool', t, self.rope_d[blk * TB:(blk + 1) * TB, :].rearrange("(s p) c -> p s c", p=128))
        return t

    def proj_tm(self, W, ncols, s, kt=8, xT=None):
        S = self.S
        xT = self.xT if xT is None else xT
        bank = [self.P[0], self.P[1], self.P[5], self.P[6]][self.pj4_i % 4]
        self.pj4_i += 1
        for k in range(kt):
            S.mm(bank[:, 0:ncols], xT[:, k, s * 128:(s + 1) * 128], W[:, k, 0:ncols], start=(k == 0), stop=(k == kt - 1))
        return bank[:, 0:ncols]

    def proj_fm(self, W, c, kt=8):
        S = self.S
        bank = [self.P[0], self.P[1], self.P[5], self.P[6]][self.pj4_i % 4]
        self.pj4_i += 1
        for k in range(kt):
            S.mm(bank[:, :], W[:, k, c * 128:(c + 1) * 128], self.xT[:, k, :], start=(k == 0), stop=(k == kt - 1))
        return bank[:, :]

    def rope(self, dst16, src32, ropet, koff):
        S = self.S
        v = src32[:].rearrange("p s (h t f) -> p s h t f", h=4, t=2)
        o = dst16[:].rearrange("p s (h t f) -> p s h t f", h=4, t=2)
        x1, x2 = v[:, :, :, 0, :], v[:, :, :, 1, :]
        o1, o2 = o[:, :, :, 0, :], o[:, :, :, 1, :]
        cos = ropet[:, :, koff:koff + 64].unsqueeze(2).to_broadcast([128, 4, 4, 64])
        sin = ropet[:, :, koff + 64:koff + 128].unsqueeze(2).to_broadcast([128, 4, 4, 64])
        ta = self.g32[4][:, 0:2, :].rearrange("p a (h f) -> p (a h) f", f=64).rearrange("p (s h) f -> p s h f", s=4)
        tb = self.g32[4][:, 2:4, :].rearrange("p a (h f) -> p (a h) f", f=64).rearrange("p (s h) f -> p s h f", s=4)
        S.tt('dve', ta, x1, cos, ALU.mult)
        S.tt('dve', tb, x2, sin, ALU.mult)
        S.tt('dve', o1, ta, tb, ALU.subtract)
        S.tt('dve', ta, x2, cos, ALU.mult)
        S.tt('dve', tb, x1, sin, ALU.mult)
        S.tt('dve', o2, ta, tb, ALU.add)

    def transpose_heads(self, dstT, src16, nh=4):
        S = self.S
        views = [self.P[2][:].bitcast(BF16), self.P[3][:].bitcast(BF16)]
        for h in range(nh):
            pt = views[h % 2]
            for s in range(NSUB):
                S.tr(pt[:, s * 128:(s + 1) * 128], src16[:, s, h * 128:(h + 1) * 128], self.identb[:])
            S.cp('act' if h % 2 == 0 else 'dve', dstT[:, h, :], pt[:, 0:512])

    def group_norm(self, o32, G, n, center, gcol, gate, out, tbase):
        S = self.S
        tn = self.tiny
        s1 = tn[:, tbase:tbase + G]
        ss = tn[:, tbase + 8:tbase + 8 + G]
        mean = tn[:, tbase + 16:tbase + 16 + G]
        var = tn[:, tbase + 24:tbase + 24 + G]
        rstd = tn[:, tbase + 32:tbase + 32 + G]
        nmr = tn[:, tbase + 40:tbase + 40 + G]
        o3 = o32.rearrange("p (g n) -> p g n", g=G)
        junk = self.g32[4][:, 0, 0:n]
        for g in range(G):
            S.act(junk, o3[:, g, :], AF.Square, accum=ss[:, g:g + 1])
        if center:
            S.rsum('dve', s1, o3)
            S.ts('dve', mean, s1, 1.0 / n, None, ALU.mult)
            S.tt('dve', var, mean, mean, ALU.mult)
            S.stt('dve', var, ss, 1.0 / n, var, ALU.mult, ALU.subtract)
        else:
            S.ts('dve', var, ss, 1.0 / n, None, ALU.mult)
        S.ts('dve', var, var, EPS, None, ALU.add)
        S.act(rstd, var, AF.Sqrt)
        S.recip(rstd, rstd)
        if center:
            S.stt('dve', nmr, mean, -1.0, rstd, ALU.mult, ALU.mult)
        for g in range(G):
            if center:
                S.act(o3[:, g, :], o3[:, g, :], AF.Identity, bias=nmr[:, g:g + 1], scale=rstd[:, g:g + 1])
            else:
                S.act(o3[:, g, :], o3[:, g, :], AF.Identity, scale=rstd[:, g:g + 1])
        if gate is not None:
            S.tt('pool', o32, o32, self.gn_g[:, gcol:gcol + G * n], ALU.mult)
            S.tt('dve', out, o32, gate, ALU.mult)
        else:
            S.tt('dve', out, o32, self.gn_g[:, gcol:gcol + G * n], ALU.mult)


    def group_norm2(self, src, o32, G, n, center, mul_ap, out, tb, junk=None):
        S = self.S
        tn = self.tiny
        ss, s1, mean, var, rstd = (tn[:, tb + 8 * i:tb + 8 * i + G] for i in range(5))
        s3 = src.rearrange("p (g n) -> p g n", g=G)
        o3 = o32.rearrange("p (g n) -> p g n", g=G)
        for g in range(G):
            S.act((self.g32[4] if junk is None else junk)[:, g % 4, 0:n], s3[:, g, :], AF.Square, accum=ss[:, g:g + 1])
        epsc = self.cs('COLS', 4, 5)
        if center:
            S.rsum('dve', s1, s3)
            S.ts('dve', mean, s1, 1.0 / n, None, ALU.mult)
            S.tt('dve', var, mean, mean, ALU.mult)
            S.stt('dve', var, ss, 1.0 / n, var, ALU.mult, ALU.subtract)
            S.act(rstd, var, AF.Ln, bias=epsc)
        else:
            S.act(rstd, ss, AF.Ln, bias=epsc, scale=1.0 / n)
        S.act(rstd, rstd, AF.Exp, scale=-0.5)
        for g in range(G):
            if center:
                S.ts('dve', o3[:, g, :], s3[:, g, :], mean[:, g:g + 1], rstd[:, g:g + 1], ALU.subtract, ALU.mult)
            else:
                S.ts('dve', o3[:, g, :], s3[:, g, :], rstd[:, g:g + 1], None, ALU.mult)
        S.tt('dve', out, o32, mul_ap, ALU.mult)

    def premul_gate(self, sg, gcol):
        self.S.tt('pool', sg[:], sg[:], self.gn_g[:, gcol:gcol + 512].unsqueeze(1).to_broadcast([128, 4, 512]), ALU.mult)

    def ret_project(self, l, blk, main):
        S = self.S
        ropet = self.load_rope(blk)
        q32, k32, sg = self.g32[0], self.g32[1], self.g32[2]
        q16, k16, v16 = self.g16[0], self.g16[1], self.g16[2]
        if main:
            W = self.next_piece(l, 0)
            for s in range(NSUB):
                S.cp('act', q32[:, s, :], self.proj_tm(W, 512, s))
        W = self.next_piece(l, 1)
        for s in range(NSUB):
            S.cp('act', k32[:, s, :], self.proj_tm(W, 512, s))
        if main:
            self.rope(q16, q32, ropet, 0)
        W = self.next_piece(l, 2)
        for s in range(NSUB):
            S.cp('act', v16[:, s, :], self.proj_tm(W, 512, s))
        self.rope(k16, k32, ropet, 128)
        if main:
            W = self.next_piece(l, 3)
            for s in range(NSUB):
                S.act(sg[:, s, :], self.proj_tm(W, 512, s), AF.Silu)
            self.premul_gate(sg, 0)
        return q16, k16, v16, sg

    def ret_state_update(self, state, kin16, v16, s, dec, pu=None):
        S = self.S
        pu = self.P[7] if pu is None else pu
        for h in range(4):
            S.mm(pu[:, h * 128:(h + 1) * 128], kin16[:, s, h * 128:(h + 1) * 128], v16[:, s, h * 128:(h + 1) * 128])
        for h in range(4):
            S.stt('dve', state[:, h, :], state[:, h, :], dec[:, h:h + 1], pu[:, h * 128:(h + 1) * 128], ALU.mult, ALU.add)

    def ret_pre(self, l, blk):
        S = self.S
        sm = self.small
        _, k16, v16, _ = self.ret_project(l, blk, False)
        kin = self.g16[3]
        wkb = sm[:, 44:48].unsqueeze(1).unsqueeze(3).to_broadcast([128, 4, 4, 128])
        S.tt('dve', kin[:].rearrange("p s (h d) -> p s h d", h=4), k16[:].rearrange("p s (h d) -> p s h d", h=4), wkb, ALU.mult)
        for s in reversed(range(NSUB)):
            n = blk * NSUB + s
            sh = self.sh_ret[n % 2]
            S.cp('act', sh[:], self.Sb_ret[:].rearrange("p h d -> p (h d)"))
            S.dma('pool', self.stsc_d[n][:, ST_RET:ST_RET + 512], sh[:], writes=[('stsc', n, 'ret')])
            self.ret_state_update(self.Sb_ret, kin, v16, s, sm[:, 52:56], pu=self.P[7 - s % 2])

    def ret_main(self, l, blk):
        S = self.S
        sm = self.small
        q16, k16, v16, sgg = self.ret_project(l, blk, True)
        qT, kT, kin = self.g16[3], self.g16[4], self.g16[5]
        qfT, qbT = self.g16[0], self.g16[1]
        wkf = sm[:, 40:44].unsqueeze(1).unsqueeze(3).to_broadcast([128, 4, 4, 128])
        S.tt('pool', kin[:].rearrange("p s (h d) -> p s h d", h=4), k16[:].rearrange("p s (h d) -> p s h d", h=4), wkf, ALU.mult)
        self.transpose_heads(kT, k16)
        views = [self.P[2][:].bitcast(BF16), self.P[3][:].bitcast(BF16)]
        for h in range(4):
            pv = views[h // 2]
            half = (h % 2) * 512
            for s in range(NSUB):
                S.tr(pv[:, half + s * 128:half + (s + 1) * 128], q16[:, s, h * 128:(h + 1) * 128], self.identb[:])
        v4 = lambda ap: ap.rearrange("p (s i) -> p s i", s=4)
        for h in range(4):
            src = views[h // 2][:, (h % 2) * 512:(h % 2) * 512 + 512]
            S.cp('act', qT[:, h, :], src)
            S.tt('dve', v4(qfT[:, h, :]), v4(qT[:, h, :]), self.efb[:, 0, h, :].unsqueeze(1).to_broadcast([128, 4, 128]), ALU.mult)
            S.tt('dve', v4(qbT[:, h, :]), v4(qT[:, h, :]), self.efb[:, 1, h, :].unsqueeze(1).to_broadcast([128, 4, 128]), ALU.mult)
        o32 = self.g32[3]
        pending = None
        for s in range(NSUB):
            n = blk * NSUB + s
            sbl = self.get_sbl(n, 'ret')
            sl = slice(s * 128, (s + 1) * 128)
            shr = self.sh_ret[s % 2][:].rearrange("p (h e) -> p h e", h=4)
            shw = self.sh_ret[(s + 1) % 2][:].rearrange("p (h e) -> p h e", h=4)
            self.ret_state_update(self.Sf_ret, kin, v16, s, sm[:, 48:52])
            S.cp('act', shw, self.Sf_ret[:])
            for h in range(4):
                S.mm(self.P[6][:, h * 128:(h + 1) * 128], kT[:, h, sl], qT[:, h, sl])
            pt16 = self.g16[6][:, s % 2, :]
            S.tt('dve', pt16.rearrange("p (h i) -> p h i", h=4), self.P[6][:].rearrange("p (h i) -> p h i", h=4), self.mc[:], ALU.mult)
            po = self.P[4 + s % 2]
            for h in range(4):
                hs = slice(h * 128, (h + 1) * 128)
                S.mm(po[:, hs], pt16[:, hs], v16[:, s, hs], start=True, stop=False)
                S.mm(po[:, hs], qfT[:, h, sl], shr[:, h, :], start=False, stop=False)
                S.mm(po[:, hs], qbT[:, h, sl], sbl[:, ST_RET + h * 128:ST_RET + (h + 1) * 128], start=False, stop=True)
            if pending is not None:
                pending()
            S.cp('act', o32[:, s, :], po[:, :])
            pending = (lambda s=s: self.group_norm2(o32[:, s, :], o32[:, s, :], 4, 128, True, sgg[:, s, :], self.mixed[:, s, 0:512], (s % 2) * 48))
        pending()

    def gates_project(self, l, blk):
        S = self.S
        gts = self.sm2[:, 0:128].rearrange("p (s c) -> p s c", s=4)
        for s in range(NSUB):
            S.cp('act', gts[:, s, :], self.proj_tm(self.wsm, 32, s))
        return gts

    def decay_prep_all(self, lf, lw, H, dirs):
        S = self.S
        H2 = 2 * H
        arr = lambda i: self.sm2[:, 128 + 64 * i:128 + 64 * i + 4 * H2].rearrange("p (s c) -> p s c", s=4)
        cum, tot, bias, dec = arr(0), arr(1), arr(2), arr(3)
        pv = self.P[2][:, 0:8 * H2].rearrange("p (s c) -> p s c", s=4)
        for s in range(NSUB):
            if 'f' in dirs:
                S.mm(pv[:, s, 0:H], self.cs('LOW'), lf[:, s, 0:H])
            if 'b' in dirs:
                S.mm(pv[:, s, H:H2], self.cs('UP'), lf[:, s, H:H2])
            S.mm(pv[:, s, H2:2 * H2], self.cs('ONES'), lf[:, s, 0:H2])
        lo, hi = (0 if 'f' in dirs else H), (H2 if 'b' in dirs else H)
        S.cp('act', cum[:, :, lo:hi], pv[:, :, lo:hi])
        S.cp('act', tot, pv[:, :, H2:2 * H2])
        S.tt('dve', bias[:, :, lo:hi], lw[:, :, lo:hi], cum[:, :, lo:hi], ALU.subtract)
        S.act(dec[:, :, lo:hi], tot[:, :, lo:hi], AF.Exp)
        S.tt('dve', tot[:, :, lo:hi], tot[:, :, lo:hi], bias[:, :, lo:hi], ALU.add)
        S.act(tot[:, :, lo:hi], tot[:, :, lo:hi], AF.Exp)
        S.act(cum[:, :, lo:hi], cum[:, :, lo:hi], AF.Exp)
        return dict(bias=bias, eq=cum, wk=tot, dec=dec)

    def decay_matrix(self, region, lfcol, bias_col, fwd, out):
        S = self.S
        S.mm(region, lfcol.to_broadcast([128, 128]), self.cs('LOW') if fwd else self.cs('UP'), start=True, stop=False)
        S.mm(region, self.identb[:], self.negfb[:] if fwd else self.negbb[:], start=False, stop=True)
        S.act(out, region, AF.Exp, bias=bias_col)

    def ssd_project(self, l, blk, src, main):
        S = self.S
        sm = self.small
        T = self.T
        ident = self.cs('IDENT')
        sz = self.g32[0]
        if main:
            W = self.next_piece(l, 14)
            for s in range(NSUB):
                S.act(sz[:, s, :], self.proj_tm(W, 512, s), AF.Silu)
        S.memset('dve', self.xh32, 0.0)
        t0 = blk * TB
        if blk > 0:
            S.dma('pool', self.g32[5][0:2, 2:4, :].rearrange("p a c -> p (a c)"), src[t0 - 2:t0, :], reads=[('hx', l, blk * NSUB - 1)])
        if blk < self.NBLK - 1:
            S.dma('pool', self.g32[5][2:4, 2:4, :].rearrange("p a c -> p (a c)"), src[t0 + TB:t0 + TB + 2, :], reads=[('hx', l, (blk + 1) * NSUB)])
        ph = self.P[3]
        for k in range(8):
            S.tr(ph[:, k * 4:(k + 1) * 4], self.xh32[:, k * 128:(k + 1) * 128], ident[0:4, 0:4])
        S.cp('act', self.xTh[:].rearrange("p k c -> p (k c)"), ph[:, 0:32])
        pre = self.pre16
        ph2 = self.P[4]
        for half in range(2):
            W = self.next_piece(l, 15 + half)
            for c in range(4):
                t = half * 4 + c
                S.cp('act', pre[:, t, 2:514], self.proj_fm(W, c))
                for k in range(8):
                    S.mm(ph2[:, t * 4:(t + 1) * 4], W[:, k, c * 128:(c + 1) * 128], self.xTh[:, k, :], start=(k == 0), stop=(k == 7))
        phv = ph2[:, 0:32].rearrange("p (t c) -> p t c", c=4)
        S.cp('dve', pre[:, :, 0:2], phv[:, :, 0:2])
        S.cp('dve', pre[:, :, 514:516], phv[:, :, 2:4])
        xsT, BCT = self.g16[2], self.g16[3]
        for t in range(8):
            if not main and t >= 6:
                break
            bank = [self.P[0], self.P[1], self.P[5], self.P[6]][self.pj4_i % 4]
            self.pj4_i += 1
            for k in range(5):
                S.mm(bank[:, :], self.dg[:, k, t, :], pre[:, t, k:k + 512], start=(k == 0), stop=(k == 4))
            dst = xsT[:, t, :] if t < 4 else BCT[:, t - 4, :]
            S.act(dst, bank[:, :], AF.Silu, bias=sm[:, 104 + t:105 + t])
        xs16, B16 = self.g16[4], self.g16[5]
        views = [self.P[2][:].bitcast(BF16), self.P[3][:].bitcast(BF16)]
        for s in range(NSUB):
            pt = views[s % 2]
            for t in range(4):
                S.tr(pt[:, t * 128:(t + 1) * 128], xsT[:, t, s * 128:(s + 1) * 128], self.identb[:])
            S.cp('act' if s % 2 == 0 else 'dve', xs16[:, s, :], pt[:, 0:512])
        for s in range(NSUB):
            pt = views[s % 2]
            for g in range(2):
                S.tr(pt[:, g * 128:(g + 1) * 128], BCT[:, g, s * 128:(s + 1) * 128], self.identb[:])
            S.cp('dve' if s % 2 == 0 else 'act', B16[:, s, 0:256], pt[:, 0:256])
        gts = self.sm2[:, 0:128].rearrange("p (s c) -> p s c", s=4)
        dt = self.sm2[:, 384:448].rearrange("p (s c) -> p s c", s=4)
        lf = self.sm2[:, 448:512].rearrange("p (s c) -> p s c", s=4)
        ldt = self.tiny[:, 128:192].rearrange("p (s c) -> p s c", s=4)
        one = self.cs('ONES', 0, 1)
        S.tt('dve', dt, gts[:, :, 16:32], sm[:, 80:96].unsqueeze(1).to_broadcast([128, 4, 16]), ALU.add)
        S.act(dt, dt, AF.Exp)
        S.act(dt, dt, AF.Ln, bias=one)
        S.act(ldt, dt, AF.Ln)
        S.tt('dve', lf, dt, sm[:, 64:80].unsqueeze(1).to_broadcast([128, 4, 16]), ALU.mult)
        return dict(sz=sz, xs16=xs16, B16=B16, BT=BCT[:, 0:2, :], CT=BCT[:, 2:4, :], lf=lf, ldt=ldt)

    def ssd_state_update(self, t, s, vs16, dec, pu=None):
        S = self.S
        pu = self.P[7] if pu is None else pu
        for g in range(2):
            S.mm(pu[:, g * 256:(g + 1) * 256], t['B16'][:, s, g * 128:(g + 1) * 128], vs16[:, g * 256:(g + 1) * 256])
        S3 = self.S_ssd[:].rearrange("p (h e) -> p h e", h=8)
        S.tt('dve', S3, S3, dec.unsqueeze(2).to_broadcast([128, 8, 64]), ALU.mult)
        S.tt('dve', self.S_ssd[:], self.S_ssd[:], pu[:, :], ALU.add)

    def ssd_vs(self, t, wk, dst):
        self.S.tt('dve', dst[:].rearrange("p s (h e) -> p s h e", h=8), t['xs16'][:].rearrange("p s (h e) -> p s h e", h=8),
                  wk.unsqueeze(3).to_broadcast([128, 4, 8, 64]), ALU.mult)

    def ssd_pre(self, l, blk, src):
        S = self.S
        t = self.ssd_project(l, blk, src, False)
        d = self.decay_prep_all(t['lf'], t['ldt'], 8, 'b')
        vs = self.g16[6]
        self.ssd_vs(t, d['wk'][:, :, 8:16], vs)
        for s in reversed(range(NSUB)):
            n = blk * NSUB + s
            sh = self.sh_ssd[n % 2]
            S.cp('act', sh[:], self.S_ssd[:])
            S.dma('pool', self.stsc_d[n][:, ST_SSD:ST_SSD + 512], sh[:], writes=[('stsc', n, 'ssd')])
            self.ssd_state_update(t, s, vs[:, s, :], d['dec'][:, s, 8:16], pu=self.P[7 - s % 2])

    def ssd_main(self, l, blk, src):
        S = self.S
        sm = self.small
        t = self.ssd_project(l, blk, src, True)
        d = self.decay_prep_all(t['lf'], t['ldt'], 8, 'fb')
        vs = self.g16[0]
        self.ssd_vs(t, d['wk'][:, :, 0:8], vs)
        xd = self.g32[2]
        S.tt('pool', xd[:].rearrange("p s (h e) -> p s h e", h=8), t['xs16'][:].rearrange("p s (h e) -> p s h e", h=8),
             sm[:, 96:104].unsqueeze(1).unsqueeze(3).to_broadcast([128, 4, 8, 64]), ALU.mult)
        E = self.g32[1]
        Ef = E[:, 0:2, :].rearrange("p a (h i) -> p (a h) i", i=128)
        Eb = E[:, 2:4, :].rearrange("p a (h i) -> p (a h) i", i=128)
        y32t, t32 = self.g32[3], self.g32[4][:, 3, :]
        v8 = lambda ap: ap.rearrange("p (h e) -> p h e", h=8)
        pending = None
        for s in range(NSUB):
            n = blk * NSUB + s
            sbl = self.get_sbl(n, 'ssd')
            sl = slice(s * 128, (s + 1) * 128)
            shr, shw = self.sh_ssd[s % 2], self.sh_ssd[(s + 1) % 2]
            self.ssd_state_update(t, s, vs[:, s, :], d['dec'][:, s, 0:8])
            S.cp('act', shw[:], self.S_ssd[:])
            for g in range(2):
                S.mm(self.P[3][:, g * 128:(g + 1) * 128], t['BT'][:, g, sl], t['CT'][:, g, sl])
            qbanks = [self.P[4], self.P[5], self.P[0], self.P[1]]
            for q in range(4):
                dirn, hb = q // 2, (q % 2) * 4
                for hh in range(4):
                    c = dirn * 8 + hb + hh
                    reg = qbanks[q][:, hh * 128:(hh + 1) * 128]
                    S.mm(reg, t['lf'][:, s, c:c + 1].to_broadcast([128, 128]), self.cs('LOW') if dirn == 0 else self.cs('UP'), start=True, stop=False)
                    S.mm(reg, self.identb[:], self.negfb[:] if dirn == 0 else self.negbb[:], start=False, stop=True)
            for q in range(4):
                dirn, hb = q // 2, (q % 2) * 4
                for hh in range(4):
                    c = dirn * 8 + hb + hh
                    S.act((Ef if dirn == 0 else Eb)[:, hb + hh, :], qbanks[q][:, hh * 128:(hh + 1) * 128], AF.Exp, bias=d['bias'][:, s, c:c + 1])
            S.tt('dve', E[:, 0:2, :], E[:, 0:2, :], E[:, 2:4, :], ALU.add)
            pt16 = self.g16[6][:, 2 * (s % 2):2 * (s % 2) + 2, :].rearrange("p a (h i) -> p (a h) i", i=128)
            S.tt('dve', pt16.rearrange("p (g h) i -> p g h i", g=2),
                 self.P[3][:, 0:256].rearrange("p (g i) -> p g i", g=2).unsqueeze(2).to_broadcast([128, 2, 4, 128]),
                 Ef.rearrange("p (g h) i -> p g h i", g=2), ALU.mult)
            for h in range(8):
                S.mm(self.P[6][:, h * 64:(h + 1) * 64], pt16[:, h, :], t['xs16'][:, s, h * 64:(h + 1) * 64])
            for g in range(2):
                S.mm(self.P[2][:, g * 256:(g + 1) * 256], t['CT'][:, g, sl], shr[:, g * 256:(g + 1) * 256])
            for g in range(2):
                S.mm(self.P[7][:, g * 256:(g + 1) * 256], t['CT'][:, g, sl], sbl[:, ST_SSD + g * 256:ST_SSD + (g + 1) * 256])
            y32 = y32t[:, s, :]
            S.tt('dve', v8(y32), v8(self.P[2][:, :]), d['eq'][:, s, 0:8].unsqueeze(2).to_broadcast([128, 8, 64]), ALU.mult)
            S.tt('dve', v8(t32), v8(self.P[7][:, :]), d['eq'][:, s, 8:16].unsqueeze(2).to_broadcast([128, 8, 64]), ALU.mult)
            S.tt('dve', t32, t32, xd[:, s, :], ALU.add)
            S.tt('dve', y32, y32, self.P[6][:, :], ALU.add)
            S.tt('dve', y32, y32, t32, ALU.add)
            if pending is not None:
                pending()

            def tail(s=s, y32=y32):
                S.tt('dve', y32, y32, t['sz'][:, s, :], ALU.mult)
                self.group_norm2(y32, y32, 2, 256, False, self.gn_g[:, 1536:2048], self.mixed[:, s, 1536:2048], (s % 2) * 48)
            pending = tail
        pending()

    def ml_project(self, l, blk, main):
        S = self.S
        sm = self.small
        q16, k16, v16 = self.g16[0], self.g16[1], self.g16[2]
        so, sg = self.g32[0], self.g32[1]
        if main:
            W = self.next_piece(l, 9)
            for s in range(NSUB):
                S.cp('act', q16[:, s, :], self.proj_tm(W, 512, s))
        W = self.next_piece(l, 10)
        for s in range(NSUB):
            S.act(k16[:, s, :], self.proj_tm(W, 512, s), AF.Identity, scale=float(128 ** -0.5))
        W = self.next_piece(l, 11)
        for s in range(NSUB):
            S.cp('act', v16[:, s, :], self.proj_tm(W, 512, s))
        if main:
            W = self.next_piece(l, 12)
            for s in range(NSUB):
                S.act(so[:, s, :], self.proj_tm(W, 512, s), AF.Sigmoid)
            W = self.next_piece(l, 13)
            for s in range(NSUB):
                S.act(sg[:, s, :], self.proj_tm(W, 512, s), AF.Silu)
        gts = self.sm2[:, 0:128].rearrange("p (s c) -> p s c", s=4)
        ip = self.tiny[:, 192:224].rearrange("p (s c) -> p s c", s=4)
        lf = self.tiny[:, 224:256].rearrange("p (s c) -> p s c", s=4)
        one = self.cs('ONES', 0, 1)
        S.tt('dve', ip, gts[:, :, 0:8], sm[:, 160:168].unsqueeze(1).to_broadcast([128, 4, 8]), ALU.add)
        S.tt('dve', lf, gts[:, :, 8:16], sm[:, 168:176].unsqueeze(1).to_broadcast([128, 4, 8]), ALU.add)
        S.act(lf, lf, AF.Exp, scale=-1.0)
        S.act(lf, lf, AF.Ln, bias=one)
        S.ts('dve', lf, lf, -1.0, None, ALU.mult)
        return dict(q16=q16, k16=k16, v16=v16, so=so, sg=sg, ip=ip, lf=lf)

    def ml_kin(self, t, wk):
        kin = self.g16[5]
        self.S.tt('dve', kin[:].rearrange("p s (h d) -> p s h d", h=4), t['k16'][:].rearrange("p s (h d) -> p s h d", h=4),
                  wk.unsqueeze(3).to_broadcast([128, 4, 4, 128]), ALU.mult)
        return kin

    def ml_state_update(self, t, kin, s, dec, pu=None):
        S = self.S
        pu = self.P[7] if pu is None else pu
        for grp in ((0, 1, 2), (3,)):
            for h in grp:
                r = (h % 3) * 129
                hs = slice(h * 128, (h + 1) * 128)
                S.mm(pu[:, r:r + 128], kin[:, s, hs], t['v16'][:, s, hs])
                S.mm(pu[:, r + 128:r + 129], kin[:, s, hs], self.ones16[:, 0:1])
            for h in grp:
                r = (h % 3) * 129
                S.stt('dve', self.S_ml[:, h, 0:129], self.S_ml[:, h, 0:129], dec[:, h:h + 1], pu[:, r:r + 129], ALU.mult, ALU.add)

    def ml_pre(self, l, blk):
        S = self.S
        t = self.ml_project(l, blk, False)
        d = self.decay_prep_all(t['lf'], t['ip'], 4, 'b')
        kin = self.ml_kin(t, d['wk'][:, :, 4:8])
        for s in reversed(range(NSUB)):
            n = blk * NSUB + s
            sh = self.sh_ml[n % 2]
            S.cp('act', sh[:].rearrange("p (h e) -> p h e", h=4)[:, :, 0:129], self.S_ml[:, :, 0:129])
            S.dma('pool', self.stsc_d[n][:, ST_ML:ST_ML + 528], sh[:], writes=[('stsc', n, 'ml')])
            self.ml_state_update(t, kin, s, d['dec'][:, s, 4:8], pu=self.P[7 - s % 2])

    def ml_main(self, l, blk):
        S = self.S
        t = self.ml_project(l, blk, True)
        self.premul_gate(t['sg'], 1024)
        qT, kT = self.g16[3], self.g16[4]
        self.transpose_heads(qT, t['q16'])
        self.transpose_heads(kT, t['k16'])
        d = self.decay_prep_all(t['lf'], t['ip'], 4, 'fb')
        kin = self.ml_kin(t, d['wk'][:, :, 0:4])
        hs32 = self.g32[2]
        flat4 = self.g32[4][:].rearrange("p a c -> p (a c)")
        num32s = [flat4[:, 0:528].rearrange("p (h e) -> p h e", h=4), flat4[:, 528:1056].rearrange("p (h e) -> p h e", h=4)]
        tmp32 = flat4[:, 1536:2048]
        Eall = self.g32[3][:].rearrange("p a c -> p (a c)")
        obanks = [self.P[4], self.P[5], self.P[6], self.P[0]]
        pending = None
        for s in range(NSUB):
            n = blk * NSUB + s
            sbl = self.get_sbl(n, 'ml')
            sl = slice(s * 128, (s + 1) * 128)
            shr = self.sh_ml[s % 2][:].rearrange("p (h e) -> p h e", h=4)
            shw = self.sh_ml[(s + 1) % 2][:].rearrange("p (h e) -> p h e", h=4)
            self.ml_state_update(t, kin, s, d['dec'][:, s, 0:4])
            S.cp('act', shw[:, :, 0:129], self.S_ml[:, :, 0:129])
            for h in range(4):
                S.mm(self.P[3][:, h * 128:(h + 1) * 128], kT[:, h, sl], qT[:, h, sl])
            dbank = [self.P[1], self.P[2]]
            for dirn in range(2):
                for h in range(4):
                    c = dirn * 4 + h
                    reg = dbank[dirn][:, h * 128:(h + 1) * 128]
                    S.mm(reg, t['lf'][:, s, c:c + 1].to_broadcast([128, 128]), self.cs('LOW') if dirn == 0 else self.cs('UP'), start=True, stop=False)
                    S.mm(reg, self.identb[:], self.negfb[:] if dirn == 0 else self.negbb[:], start=False, stop=True)
            for dirn in range(2):
                for h in range(4):
                    c = dirn * 4 + h
                    S.act(Eall[:, c * 128:(c + 1) * 128], dbank[dirn][:, h * 128:(h + 1) * 128], AF.Exp, bias=d['bias'][:, s, c:c + 1])
            for dirn in range(2):
                S.tt('dve', self.g16[6][:, 2 * (s % 2) + dirn, :], self.P[3][:, :], Eall[:, dirn * 512:(dirn + 1) * 512], ALU.mult)
            for dirn in range(2):
                pt16 = self.g16[6][:, 2 * (s % 2) + dirn, :]
                num32 = num32s[dirn]
                obanks = [self.P[4], self.P[5], self.P[6], self.P[0]] if dirn == 0 else [self.P[1], self.P[2], self.P[3], self.P[7]]
                for h in range(4):
                    hs = slice(h * 128, (h + 1) * 128)
                    bank = obanks[h]
                    S.mm(bank[:, 0:128], pt16[:, hs], t['v16'][:, s, hs])
                    S.mm(bank[:, 128:129], pt16[:, hs], self.ones16[:, 0:1])
                    S16 = shr[:, h, 0:129] if dirn == 0 else sbl[:, ST_ML + h * 132:ST_ML + h * 132 + 129]
                    S.mm(bank[:, 129:258], qT[:, h, sl], S16)
                for h in range(4):
                    S.cp('act', num32[:, h, 0:129], obanks[h][:, 0:129])
                for h in range(4):
                    c = dirn * 4 + h
                    S.stt('dve', num32[:, h, 0:129], obanks[h][:, 129:258], d['eq'][:, s, c:c + 1], num32[:, h, 0:129], ALU.mult, ALU.add)
                dn = self.tiny[:, 104 + dirn * 4:108 + dirn * 4]
                S.act(dn.unsqueeze(2), num32[:, :, 128:129], AF.Abs)
                S.ts('dve', dn, dn, 1.0, None, ALU.max)
                S.recip(dn, dn)
                h3 = hs32[:, s, :].rearrange("p (h e) -> p h e", h=4)
                if dirn == 0:
                    S.tt('dve', h3, num32[:, :, 0:128], dn.unsqueeze(2).to_broadcast([128, 4, 128]), ALU.mult)
                else:
                    S.tt('dve', tmp32.rearrange("p (h e) -> p h e", h=4), num32[:, :, 0:128], dn.unsqueeze(2).to_broadcast([128, 4, 128]), ALU.mult)
                    S.tt('dve', hs32[:, s, :], hs32[:, s, :], tmp32, ALU.add)
            if pending is not None:
                pending()

            def tail(s=s):
                S.tt('dve', hs32[:, s, :], hs32[:, s, :], t['so'][:, s, :], ALU.mult)
                self.group_norm2(hs32[:, s, :], hs32[:, s, :], 4, 128, True, t['sg'][:, s, :], self.mixed[:, s, 1024:1536], (s % 2) * 48, junk=self.g32[5])
            pending = tail
        pending()

    def hg_lb_setup(self):
        S = self.S
        lg = self.tiny[:, 0:16]
        for l in range(DEPTH):
            for dd in range(2):
                S.dma('pool', lg[:, l * 8 + dd * 4:l * 8 + dd * 4 + 4], self.hg_lb_d[l, dd].rearrange("(t p) -> p t", p=128),
                      allow_slow_non_contiguous=True)
        e = self.tiny[:, 16:32]
        S.act(e, lg, AF.Exp)
        ssum = self.tiny[:, 32:40]
        S.tt('dve', ssum, e[:, 0:8], e[:, 8:16], ALU.add)
        S.recip(ssum, ssum)
        w = self.tiny[:, 40:56]
        S.tt('dve', w[:, 0:8], e[:, 0:8], ssum, ALU.mult)
        S.tt('dve', w[:, 8:16], e[:, 8:16], ssum, ALU.mult)
        lb = self.hglb[:, 0:16]
        S.tt('dve', lb[:, 0:8], w[:, 0:8], w[:, 0:8], ALU.subtract)
        S.tt('dve', lb[:, 8:16], w[:, 0:8], w[:, 8:16], ALU.add)
        S.tt('dve', lb[:, 8:16], lb[:, 8:16], w[:, 0:8], ALU.subtract)
        S.ts('dve', self.hglb[:, 16:32], lb, -1.0, 1.0, ALU.mult, ALU.add)

    def hg_gate_head(self, l, W, dirn, h, main, qT32, qq, kk, kin, slot):
        S = self.S
        if slot == 0:
            A, B, D_, E_ = (self.g32[3][:, i, :] for i in range(4))
            Cc = self.g32[4][:, 0:2, :].rearrange("p a c -> p (a c)")[:, 0:513]
            F_ = self.g32[4][:, 2, :]
        else:
            A, B, D_, E_ = (self.g32[5][:, i, :] for i in range(4))
            Cc = self.g32[2][:, 0:2, :].rearrange("p a c -> p (a c)")[:, 0:513]
            F_ = self.g32[2][:, 2, :]
        kT16 = self.g16[6][:, slot, :]
        col = l * 8 + dirn * 4 + h
        lbc, omc = self.hglb[:, col:col + 1], self.hglb[:, 16 + col:17 + col]
        v3 = lambda ap: ap.rearrange("p (a i) -> p a i", i=64)
        ps = self.proj_fm(W, h)
        one = self.cs('ONES', 0, 1)
        S.act(A, ps, AF.Exp, scale=-1.0)
        yield
        S.act(B, A, AF.Ln, bias=one, scale=lbc)
        S.act(D_, A, AF.Ln, bias=one)
        yield
        S.tt('dve', B, B, D_, ALU.subtract)
        yield
        S.act(A, B, AF.Exp)
        yield
        S.ts('dve', kT16, A, -1.0, 1.0, ALU.mult, ALU.add)
        S.memset('dve', Cc[:, 0:1], 0.0)
        yield
        S.scan(Cc[:, 1:513], self.cs('ONES', 0, 1).to_broadcast([128, 512]), B, 0.0, ALU.mult, ALU.add)
        yield
        if dirn == 0:
            S.tt('dve', v3(D_), v3(Cc[:, 1:513]), Cc[:, 0:512:64].unsqueeze(2).to_broadcast([128, 8, 64]), ALU.subtract)
        else:
            S.tt('dve', v3(D_), Cc[:, 64:513:64].unsqueeze(2).to_broadcast([128, 8, 64]), v3(Cc[:, 1:513]), ALU.subtract)
            S.tt('dve', D_, D_, B, ALU.add)
        yield
        S.act(E_, D_, AF.Exp)
        S.act(F_, D_, AF.Exp, scale=-1.0)
        yield
        dcol = 63 if dirn == 0 else 0
        dsc = self.hgdec[:, (dirn * 4 + h) * 8:(dirn * 4 + h) * 8 + 8]
        S.cp('dve', dsc, E_[:, dcol:512:64])
        if main:
            S.tt('dve', qq[dirn][:, h, :], qT32[:, h, :], E_, ALU.mult)
        yield
        S.tt('pool', kk[dirn][:, h, :], kT16, F_, ALU.mult)
        if (main and dirn == 0) or (not main and dirn == 1):
            S.tt('pool', v3(kin[:, h, :]), v3(kk[dirn][:, h, :]), dsc.unsqueeze(2).to_broadcast([128, 8, 64]), ALU.mult)

    @staticmethod
    def run_interleaved(gens):
        gens = list(gens)
        while gens:
            for g in list(gens):
                try:
                    next(g)
                except StopIteration:
                    gens.remove(g)

    def hg_project(self, l, blk, main):
        S = self.S
        qT32, sg = self.g32[0], self.g32[1]
        qq, kk, kin, v16 = [self.g16[0], self.g16[1]], [self.g16[2], self.g16[3]], self.g16[4], self.g16[5]
        if main:
            W = self.next_piece(l, 4)
            for h in range(4):
                S.cp('act', qT32[:, h, :], self.proj_fm(W, h))
        for dirn in ([0, 1] if main else [1]):
            W = self.next_piece(l, 5 + dirn)
            for hp in (0, 2):
                self.run_interleaved([self.hg_gate_head(l, W, dirn, hp + j, main, qT32, qq, kk, kin, j) for j in range(2)])
        W = self.next_piece(l, 7)
        for s in range(NSUB):
            S.cp('act', v16[:, s, :], self.proj_tm(W, 512, s))
        if main:
            W = self.next_piece(l, 8)
            for s in range(NSUB):
                S.act(sg[:, s, :], self.proj_tm(W, 512, s), AF.Silu)
        return dict(qq=qq, kk=kk, kin=kin, v16=v16, sg=sg)

    def hg_kinT_all(self, t):
        S = self.S
        kinT = self.g16[6]
        views = [self.P[2][:].bitcast(BF16), self.P[3][:].bitcast(BF16)]
        for s in range(NSUB):
            pt = views[s % 2]
            for r in range(2):
                a = 2 * s + r
                for h in range(4):
                    S.tr(pt[r * 64:(r + 1) * 64, h * 128:(h + 1) * 128], t['kin'][:, h, a * 64:(a + 1) * 64], self.identb[:])
            S.cp('act' if s % 2 == 0 else 'dve', kinT[:, s, :], pt[:, 0:512])
        return kinT

    def hg_state_update(self, t, kinT, s, r, dirn, pu=None):
        S = self.S
        a = 2 * s + r
        pr = slice(r * 64, (r + 1) * 64)
        pu = self.P[7] if pu is None else pu
        for h in range(4):
            hs = slice(h * 128, (h + 1) * 128)
            S.mm(pu[:, hs], kinT[pr, s, hs], t['v16'][pr, s, hs])
        for h in range(4):
            c = (dirn * 4 + h) * 8 + a
            S.stt('dve', self.S_hg[:, h, :], self.S_hg[:, h, :], self.hgdec[:, c:c + 1], pu[:, h * 128:(h + 1) * 128], ALU.mult, ALU.add)

    def hg_pre(self, l, blk):
        S = self.S
        t = self.hg_project(l, blk, False)
        kinT = self.hg_kinT_all(t)
        for s in reversed(range(NSUB)):
            n = blk * NSUB + s
            for r in (1, 0):
                a = 2 * s + r
                sh = self.sh_hg[a % 2]
                S.cp('act', sh[:], self.S_hg[:].rearrange("p h e -> p (h e)"))
                S.dma('pool', self.stsc_d[n][:, ST_HG0 + r * 512:ST_HG0 + (r + 1) * 512], sh[:], writes=[('stsc', n, 'hg%d' % r)])
                self.hg_state_update(t, kinT, s, r, 1, pu=self.P[7 - r])

    def hg_main(self, l, blk):
        S = self.S
        t = self.hg_project(l, blk, True)
        self.premul_gate(t['sg'], 512)
        kinT = self.hg_kinT_all(t)
        qq, kk = t['qq'], t['kk']
        o32 = self.g32[2]
        mo, mw = CST['LOW64']
        pending = None
        for s in range(NSUB):
            n = blk * NSUB + s
            sbl = self.get_sbl(n, 'hg')
            po = self.P[4 + s % 2]
            for r in range(2):
                a = 2 * s + r
                asl = slice(a * 64, (a + 1) * 64)
                pr = slice(r * 64, (r + 1) * 64)
                shr = self.sh_hg[a % 2][:].rearrange("p (h e) -> p h e", h=4)
                shw = self.sh_hg[(a + 1) % 2][:].rearrange("p (h e) -> p h e", h=4)
                self.hg_state_update(t, kinT, s, r, 0)
                S.cp('act', shw, self.S_hg[:])
                for dirn in range(2):
                    for h in range(4):
                        c = dirn * 4 + h
                        S.mm(self.P[3][pr, c * 64:(c + 1) * 64], kk[dirn][:, h, asl], qq[dirn][:, h, asl])
                PTm = self.g32[5][pr, 0, :]
                mask = self.cst[pr, mo:mo + 128].rearrange("p (d i) -> p d i", d=2).unsqueeze(2).to_broadcast([64, 2, 4, 64])
                S.tt('dve', PTm.rearrange("p (d h i) -> p d h i", d=2, h=4), self.P[3][pr, :].rearrange("p (d h i) -> p d h i", d=2, h=4), mask, ALU.mult)
                PT16 = self.g16[4][pr, 0, 0:256]
                S.tt('dve', PT16, PTm[:, 0:256], PTm[:, 256:512], ALU.add)
                for h in range(4):
                    hs = slice(h * 128, (h + 1) * 128)
                    S.mm(po[pr, hs], PT16[:, h * 64:(h + 1) * 64], t['v16'][pr, s, hs], start=True, stop=False)
                    S.mm(po[pr, hs], qq[0][:, h, asl], shr[:, h, :], start=False, stop=False)
                    S.mm(po[pr, hs], qq[1][:, h, asl], sbl[:, ST_HG0 + r * 512 + h * 128:ST_HG0 + r * 512 + (h + 1) * 128], start=False, stop=True)
            if pending is not None:
                pending()
            pending = (lambda po=po, s=s: self.group_norm2(po[:, :], o32[:, s, :], 4, 128, False, t['sg'][:, s, :], self.mixed[:, s, 512:1024], (s % 2) * 48))
        pending()

    def get_sbl(self, n, m):
        if not hasattr(self, '_sbl_held'):
            self._sbl_held = {}
        for nn in (n, n + 1):
            if nn >= self.NCH:
                continue
            if self._sbl_held.get((nn % 2, m)) != (self.cur_layer, nn):
                self._load_sbl(nn, m)
        return self.sbl[n % 2]

    def _load_sbl(self, n, m):
        t = self.sbl[n % 2]
        regs = {'ret': (ST_RET, ST_HG0), 'hg': (ST_HG0, ST_ML), 'ml': (ST_ML, ST_SSD), 'ssd': (ST_SSD, NSTATE)}
        rk = [('stsc', n, 'hg0'), ('stsc', n, 'hg1')] if m == 'hg' else [('stsc', n, m)]
        a, b = regs[m]
        self.S.dma('pool', t[:, a:b], self.stsc_d[n][:, a:b], reads=rk)
        self._sbl_held[(n % 2, m)] = (self.cur_layer, n)

    def pre_block(self, l, blk, src):
        self.cur_layer = l
        self.in_main = False
        self.load_xT(blk, src)
        if 'ret' in self.mixers:
            self.ret_pre(l, blk)
        if 'hg' in self.mixers:
            self.hg_pre(l, blk)
        if 'ml' in self.mixers or 'ssd' in self.mixers:
            self.gates_project(l, blk)
        if 'ml' in self.mixers:
            self.ml_pre(l, blk)
        if 'ssd' in self.mixers:
            self.ssd_pre(l, blk, src)

    def main_block(self, l, blk, src, dst):
        S = self.S
        self.cur_layer = l
        self.in_main = True
        self.load_xT(blk, src)
        if 'ret' in self.mixers:
            self.ret_main(l, blk)
        if 'hg' in self.mixers:
            self.hg_main(l, blk)
        if 'ml' in self.mixers or 'ssd' in self.mixers:
            self.gates_project(l, blk)
        if 'ml' in self.mixers:
            self.ml_main(l, blk)
        if 'ssd' in self.mixers:
            self.ssd_main(l, blk, src)
        if self.debug == 'mixed' or (self.debug == 'mixed1' and l == 1):
            for s in range(NSUB):
                self.dump(self.mixed[:, s, :], blk * NSUB + s, 2048)
            for pid in (17, 18, 19, 20, 21, 22):
                self.next_piece(l, pid)
            return
        self.out_stage(l, blk, src, dst)

    def dump(self, ap, n, width=512):
        S = self.S
        if 'dbg' not in self.dbg_outs:
            self.dbg_outs['dbg'] = self.dout("dbg", [self.T, width])
        for c0 in range(0, width, 512):
            t = self.g32[5][:, 3, :]
            S.cp('dve', t, ap[:, c0:c0 + 512])
            S.dma('pool', self.dbg_outs['dbg'][n * 128:(n + 1) * 128, c0:c0 + 512], t)

    def out_stage(self, l, blk, src, dst):
        S = self.S
        views = [self.P[2][:].bitcast(BF16), self.P[3][:].bitcast(BF16)]
        mT = lambda kt: self.g16[kt // 4][:, kt % 4, :]
        for kt in range(16):
            pt = views[kt % 2]
            for s in range(NSUB):
                S.tr(pt[:, s * 128:(s + 1) * 128], self.mixed[:, s, kt * 128:(kt + 1) * 128], self.identb[:])
            S.cp('act' if kt % 2 == 0 else 'dve', mT(kt), pt[:, 0:512])
        self.load_x32(blk, src, l)
        ident = self.cs('IDENT')
        p32 = self.g32[5][:, :, 0:256]
        S.dma('pool', p32, self.p_d[l][blk * TB:(blk + 1) * TB, :].rearrange("(s p) c -> p s c", p=128))
        pT = self.g16[6][:, 0:2, :]
        for k in range(2):
            bank = self.P[3 + (k % 2)]
            for s in range(NSUB):
                S.tr(bank[:, s * 128:(s + 1) * 128], p32[:, s, k * 128:(k + 1) * 128], ident)
            S.cp('act', pT[:, k, :], bank[:, :])
        zt = lambda s: self.g32[s // 2][:, 2 * (s % 2):2 * (s % 2) + 2, :].rearrange("p a c -> p (a c)")
        for ch in range(2):
            WW = [self.next_piece(l, 17 + 2 * ch), self.next_piece(l, 18 + 2 * ch)]
            for s in range(NSUB):
                bank = [self.P[0], self.P[1], self.P[7]][self.pj3_i % 3]
                self.pj3_i += 1
                for kt in range(16):
                    S.mm(bank[:, :], mT(kt)[:, s * 128:(s + 1) * 128], WW[kt // 8][:, kt % 8, :], start=(kt == 0), stop=(kt == 15))
                S.stt('dve', zt(s)[:, ch * 512:(ch + 1) * 512], self.xs32(s)[:, ch * 512:(ch + 1) * 512], float(DN_ALPHA), bank[:, :], ALU.mult, ALU.add)
        tn = self.tiny
        xnT = (self.g16[4], self.g16[5])
        if blk + 1 < self.NBLK:
            self.load_x32(blk + 1, src, l)
            self.x_prefetched = (l, blk + 1, True)
        st = lambda i: tn[:, 96 + 4 * i:100 + 4 * i]
        s1, ss, mean, var, rstd, nmr = (st(i) for i in range(6))
        epsc = self.cs('COLS', 4, 5)
        for s in range(NSUB):
            junk = self.g32[4 + s // 2][:, 2 * (s % 2):2 * (s % 2) + 2, :].rearrange("p a c -> p (a c)")
            S.act(junk, zt(s), AF.Square, accum=ss[:, s:s + 1])
        for s in range(NSUB):
            S.rsum('dve', s1[:, s:s + 1], zt(s))
        S.ts('dve', mean, s1, 1.0 / DM, None, ALU.mult)
        S.tt('dve', var, mean, mean, ALU.mult)
        S.stt('dve', var, ss, 1.0 / DM, var, ALU.mult, ALU.subtract)
        S.act(rstd, var, AF.Ln, bias=epsc)
        S.act(rstd, rstd, AF.Exp, scale=-0.5)
        S.stt('dve', nmr, mean, -1.0, rstd, ALU.mult, ALU.mult)
        for s in range(NSUB):
            S.act(zt(s), zt(s), AF.Identity, bias=nmr[:, s:s + 1], scale=rstd[:, s:s + 1])
        for s in range(NSUB):
            S.tt('pool' if s % 2 else 'dve', zt(s), zt(s), self.ln_g[:], ALU.mult)
        for s in range(NSUB):
            S.tt('dve', zt(s), zt(s), self.ln_b[:], ALU.add)
        for k in range(8):
            bank = self.P[3 + (k % 2)]
            for s in range(NSUB):
                S.tr(bank[:, s * 128:(s + 1) * 128], zt(s)[:, k * 128:(k + 1) * 128], ident)
            S.cp('act' if k % 2 == 0 else 'dve', xnT[k // 4][:, k % 4, :], bank[:, :])
        o32 = self.g32[4], self.g32[5]
        ot = lambda s: o32[s // 2][:, 2 * (s % 2):2 * (s % 2) + 2, :].rearrange("p a c -> p (a c)")
        for ch in range(2):
            Wg = self.next_piece(l, 21 + ch)
            for s in range(NSUB):
                bank = [self.P[0], self.P[1], self.P[7]][self.pj3_i % 3]
                self.pj3_i += 1
                self.pj_i += 1
                for k in range(8):
                    S.mm(bank[:, :], xnT[k // 4][:, k % 4, s * 128:(s + 1) * 128], Wg[:, k, :], start=(k == 0), stop=(k == 7))
                gt = ot(s)[:, ch * 512:(ch + 1) * 512]
                S.act(gt, bank[:, :], AF.Sigmoid)
                bank2 = self.P[5 + (self.pj_i % 2)]
                for k in range(2):
                    S.mm(bank2[:, :], pT[:, k, s * 128:(s + 1) * 128], self.wpp[:, k, ch * 512:(ch + 1) * 512], start=(k == 0), stop=(k == 1))
                S.tt('dve', gt, gt, bank2[:, :], ALU.mult)
                S.tt('pool', gt, gt, zt(s)[:, ch * 512:(ch + 1) * 512], ALU.add)
        for s in range(NSUB):
            S.dma('pool', dst[blk * TB + s * 128:blk * TB + (s + 1) * 128, :], ot(s),
                  writes=[('hx', l + 1, blk * NSUB + s)])


def build_program(T, **kw):
    pr = Prog(T, **kw)
    nc = pr.build()
    return nc, pr


_CACHE = {}


def make_in_map(x, p, prm, cst, rope):
    f = lambda a: np.ascontiguousarray(a, dtype=np.float32)
    return {
        "x": f(x), "pp": f(p),
        "w_in": f(prm['w_in']), "w_out": f(prm['w_out']),
        "w_ple_gate": f(prm['w_ple_gate']), "w_ple_proj": f(prm['w_ple_proj']),
        "ln_g": f(prm['ln_g']), "ln_b": f(prm['ln_b']),
        "ret_log_rate": f(prm['ret_log_rate']).reshape(DEPTH, 8),
        "gn_g": f(np.concatenate([prm['ret_norm_g'], prm['hgrn_norm_g'], prm['mlstm_norm_g'], prm['ssd_norm_g']], axis=1)),
        "hgrn_lb_logits": f(prm['hgrn_lb_logits']),
        "ml_bias": f(np.concatenate([np.reshape(prm['mlstm_i_bias'], (DEPTH, 8)), np.reshape(prm['mlstm_f_bias'], (DEPTH, 8))], axis=1)),
        "ssd_conv_w": f(prm['ssd_conv_w']), "ssd_conv_b": f(prm['ssd_conv_b']),
        "ssd_a_log": f(prm['ssd_a_log']).reshape(DEPTH, 16), "ssd_dt_bias": f(prm['ssd_dt_bias']).reshape(DEPTH, 16),
        "ssd_d": f(prm['ssd_d']),
        "cst": cst, "rope": rope,
    }


def kernel(x_prompt, x_sample, p_prompt, p_sample, **prm):
    x_prompt = np.asarray(x_prompt)
    x_sample = np.asarray(x_sample)
    p_prompt = np.asarray(p_prompt)
    p_sample = np.asarray(p_sample)
    prm = {k: np.asarray(v) for k, v in prm.items()}
    T = x_prompt.shape[1]
    nb, ns = x_prompt.shape[0], x_sample.shape[0]
    seqs = [(x_prompt[i], p_prompt[:, i]) for i in range(nb)] + [(x_sample[i], p_sample[:, i]) for i in range(ns)]
    ncores = 8
    if T not in _CACHE:
        _CACHE[T] = build_program(T)[0]
    nc = _CACHE[T]
    cst, rope = host_consts(T)
    in_maps = []
    for c in range(ncores):
        xs, ps_ = seqs[c % len(seqs)]
        in_maps.append(make_in_map(xs, ps_, prm, cst, rope))
    res = run_bass_kernel_spmd(nc, in_maps, core_ids=list(range(ncores)))
    outs = [np.asarray(res.results[c]["y"], dtype=np.float32) for c in range(len(seqs))]
    y_prompt = np.stack(outs[:nb], axis=0)
    y_sample = np.stack(outs[nb:], axis=0)
    return (y_prompt, y_sample)
```

```python
import math
from contextlib import ExitStack

import numpy as np

import concourse.bass as bass
import concourse.mybir as mybir
from concourse.bass_utils import run_bass_kernel_spmd

F32 = mybir.dt.float32
BF16 = mybir.dt.bfloat16
AF = mybir.ActivationFunctionType
ALU = mybir.AluOpType
AX = mybir.AxisListType

C = 128
NSUB = 4
TB = 512
DM = 1024
DP = 8736
DEPTH = 2
DPLE = 256
RET0, HG0, ML0, SS0 = 0, 2048, 4608, 7184
EPS = 1e-5
DN_ALPHA = (2 * DEPTH) ** 0.25
NSTATE = 2576
ST_RET, ST_HG0, ST_HG1, ST_ML, ST_SSD = 0, 512, 1024, 1536, 2064
NPIECE = 23


def _dtsize(dt):
    if dt == F32:
        return 4
    if dt == BF16:
        return 2
    return 4


class Sched:
    NRING = 8

    def __init__(self, nc, stack):
        self.nc = nc
        self.eng = {'pe': nc.tensor, 'dve': nc.vector, 'act': nc.scalar, 'pool': nc.gpsimd, 'sp': nc.sync}
        self.prog, self.cnt, self.seen, self.snaps = {}, {}, {}, {}
        for e in self.eng:
            self.prog[e] = stack.enter_context(nc.semaphore('pg_' + e))
            self.cnt[e] = 0
            self.seen[e] = {}
            self.snaps[('c', e)] = {}
        self.dsem, self.dcnt = {}, {}
        for q in ('sp', 'act', 'pool'):
            self.dsem[q] = [stack.enter_context(nc.semaphore('dq_%s_%d' % (q, i))) for i in range(self.NRING)]
            self.dcnt[q] = 0
            for i in range(self.NRING):
                self.snaps[('d', q, i)] = {}
        self.lastw, self.reads = {}, {}
        self.partbytes = {}
        self.psum_names = set()
        self.nwaits = 0
        self.nops = 0

    def keys(self, ap):
        if isinstance(ap, (tuple, str)):
            return [ap]
        name = ap.name
        ps = self.partbytes.get(name)
        if ps is None:
            return [name]
        esz = _dtsize(ap.dtype)
        a = ap.ap
        pst = a[0][0]
        off = ap.offset % pst if pst > 0 else ap.offset
        ext = 1
        for st, cn in a[1:]:
            ext += abs(st) * (cn - 1)
        lo = off * esz
        hi = (off + ext) * esz
        return [(name, i) for i in range(lo // ps, (hi - 1) // ps + 1)]

    def _sem(self, semkey):
        if semkey[0] == 'c':
            return self.prog[semkey[1]]
        return self.dsem[semkey[1]][semkey[2]]

    def _need(self, e, ev, out):
        semkey, val = ev
        if semkey[0] == 'c' and semkey[1] == e and e == 'pe':
            return
        if self.seen[e].get(semkey, 0) >= val:
            return
        if out.get(semkey, 0) < val:
            out[semkey] = val

    def _deps(self, e, rk, wk):
        need = {}
        for r in rk:
            ev = self.lastw.get(r)
            if ev is not None:
                self._need(e, ev, need)
        for w in wk:
            ev = self.lastw.get(w)
            if ev is not None:
                self._need(e, ev, need)
            for ev in self.reads.get(w, ()):
                self._need(e, ev, need)
        return need

    def _emit_waits(self, e, need):
        for semkey, val in need.items():
            if self.seen[e].get(semkey, 0) >= val:
                continue
            self.eng[e].wait_ge(self._sem(semkey), val)
            self.nwaits += 1
            new = dict(self.seen[e])
            new[semkey] = val
            snap = self.snaps[semkey].get(val)
            if snap:
                for k, v in snap.items():
                    if new.get(k, 0) < v:
                        new[k] = v
            self.seen[e] = new

    def _record(self, ev, rk, wk):
        for r in rk:
            lst = self.reads.setdefault(r, [])
            lst[:] = [x for x in lst if x[0] != ev[0]]
            lst.append(ev)
        for w in wk:
            self.lastw[w] = ev
            self.reads[w] = []

    def _is_psum(self, ap):
        return (not isinstance(ap, (tuple, str))) and ap.name in self.psum_names

    def _rw(self, reads, writes):
        rk, wk, rrec, wrec = [], [], [], []
        for r in reads:
            if r is None or isinstance(r, (int, float)):
                continue
            k = self.keys(r)
            if self._is_psum(r):
                rk.append(('bank', r.name))
                rrec += k
            else:
                rk += k
                rrec += k
        for w in writes:
            k = self.keys(w)
            wk += k
            wrec += k
            if self._is_psum(w):
                wrec.append(('bank', w.name))
        return rk, wk, rrec, wrec

    def op(self, e, fn, reads=(), writes=()):
        rk, wk, rrec, wrec = self._rw(reads, writes)
        self._emit_waits(e, self._deps(e, rk, wk))
        ins = fn(self.eng[e])
        self.cnt[e] += 1
        ins.then_inc(self.prog[e], 1)
        ev = (('c', e), self.cnt[e])
        self.snaps[('c', e)][self.cnt[e]] = self.seen[e]
        self._record(ev, rrec, wrec)
        self.nops += 1
        return ins

    def dma(self, q, out, in_, reads=None, writes=None, **kw):
        rk, wk, rrec, wrec = self._rw([in_] if reads is None else reads, [out] if writes is None else writes)
        n = self.dcnt[q]
        k = n % self.NRING
        prev = 16 * (n // self.NRING)
        need = self._deps(q, rk, wk)
        semkey = ('d', q, k)
        if prev > 0 and self.seen[q].get(semkey, 0) < prev:
            need[semkey] = max(need.get(semkey, 0), prev)
        self._emit_waits(q, need)
        ins = self.eng[q].dma_start(out=out, in_=in_, **kw)
        val = prev + 16
        ins.then_inc(self.dsem[q][k], 16)
        self.dcnt[q] = n + 1
        ev = (semkey, val)
        self.snaps[semkey][val] = self.seen[q]
        self._record(ev, rrec, wrec)
        self.nops += 1
        return ev

    def finish(self):
        for q in ('sp', 'act', 'pool'):
            n = self.dcnt[q]
            for k in range(self.NRING):
                cnt_k = (n - k + self.NRING - 1) // self.NRING if n > k else 0
                if cnt_k > 0:
                    self.eng[q].wait_ge(self.dsem[q][k], 16 * cnt_k)

    def mm(self, out, lhsT, rhs, start=True, stop=True):
        return self.op('pe', lambda e: e.matmul(out, lhsT, rhs, start=start, stop=stop),
                       reads=[lhsT, rhs], writes=[out])

    def tr(self, out, in_, ident):
        return self.op('pe', lambda e: e.transpose(out, in_, ident), reads=[in_, ident], writes=[out])

    def act(self, out, in_, func, bias=None, scale=None, accum=None, eng='act'):
        kw = {}
        if bias is not None:
            kw['bias'] = bias
        if scale is not None:
            kw['scale'] = scale
        if accum is not None:
            kw['accum_out'] = accum
        w = [out] + ([accum] if accum is not None else [])
        return self.op(eng, lambda e: e.activation(out, in_, func, **kw), reads=[in_, bias, scale], writes=w)

    def tt(self, eng, out, a, b, op):
        return self.op(eng, lambda e: e.tensor_tensor(out, a, b, op), reads=[a, b], writes=[out])

    def ts(self, eng, out, a, s1, s2, op0, op1=None):
        if op1 is None:
            return self.op(eng, lambda e: e.tensor_scalar(out, a, s1, None, op0), reads=[a, s1], writes=[out])
        return self.op(eng, lambda e: e.tensor_scalar(out, a, s1, s2, op0, op1), reads=[a, s1, s2], writes=[out])

    def stt(self, eng, out, a, s, b, op0, op1):
        return self.op(eng, lambda e: e.scalar_tensor_tensor(out, a, s, b, op0, op1), reads=[a, s, b], writes=[out])

    def cp(self, eng, out, in_):
        if eng == 'act':
            return self.op('act', lambda e: e.copy(out, in_), reads=[in_], writes=[out])
        return self.op(eng, lambda e: e.tensor_copy(out, in_), reads=[in_], writes=[out])

    def memset(self, eng, out, val):
        return self.op(eng, lambda e: e.memset(out, val), writes=[out])

    def rsum(self, eng, out, in_):
        return self.op(eng, lambda e: e.reduce_sum(out, in_, AX.X), reads=[in_], writes=[out])

    def recip(self, out, in_):
        return self.op('dve', lambda e: e.reciprocal(out, in_), reads=[in_], writes=[out])

    def scan(self, out, d0, d1, init, op0, op1):
        return self.op('dve', lambda e: e.tensor_tensor_scan(out, d0, d1, init, op0, op1), reads=[d0, d1], writes=[out])


CST = {}


def _cst_layout():
    off = 0
    for name, w in (('IDENT', 128), ('LOW', 128), ('UP', 128), ('ONES', 128), ('DIFF', 128),
                    ('NEGF', 128), ('NEGB', 128), ('COLS', 8), ('LOW64', 64), ('UP64', 64), ('ROWF', 128), ('ROWB', 128)):
        CST[name] = (off, w)
        off += w
    return off


NCST = _cst_layout()


def host_consts(T):
    p = np.arange(128)[:, None].astype(np.float64)
    i = np.arange(128)[None, :].astype(np.float64)
    c = np.zeros((128, NCST), np.float32)

    def put(name, arr):
        o, w = CST[name]
        c[:, o:o + w] = arr

    put('IDENT', (p == i))
    put('LOW', (i >= p))
    put('UP', (i <= p))
    put('ONES', np.ones((128, 128)))
    put('DIFF', i - p)
    put('NEGF', np.where(i >= p, 0.0, -1e30))
    put('NEGB', np.where(i <= p, 0.0, -1e30))
    cols = np.zeros((128, 8))
    j = np.arange(128)
    cols[:, 0] = j + 1
    cols[:, 1] = C - j
    cols[:, 2] = C - 1 - j
    cols[:, 3] = j
    cols[:, 4] = EPS
    put('COLS', cols)
    i64 = np.arange(64)[None, :]
    p64 = (np.arange(128) % 64)[:, None]
    put('LOW64', (i64 >= p64))
    put('UP64', (i64 <= p64))
    put('ROWF', np.broadcast_to(i + 1, (128, 128)))
    put('ROWB', np.broadcast_to(C - i, (128, 128)))
    d = 128
    inv = 10000.0 ** (-np.arange(0, d, 2, dtype=np.float32) / d)
    ang = np.arange(T, dtype=np.float32)[:, None] * inv[None, :]
    cos = np.cos(ang).astype(np.float32)
    sin = np.sin(ang).astype(np.float32)
    s = np.float32(128 ** -0.5)
    rope = np.concatenate([cos, sin, cos * s, sin * s], axis=1).astype(np.float32)
    return c, rope


class Prog:
    def __init__(self, T, debug=None, mixers=('ret', 'hg', 'ml', 'ssd'), nlayers=DEPTH):
        self.T = T
        self.NBLK = T // TB
        self.NCH = T // C
        self.debug = debug
        self.mixers = mixers
        self.nlayers = nlayers
        self.nc = bass.Bass("TRN2", target_bir_lowering=False)
        self.stack = ExitStack()
        self.dbg_outs = {}

    def sb(self, name, shape, dt=F32, part=None):
        t = self.stack.enter_context(self.nc.sbuf_tensor(name, shape, dt))
        if part is not None:
            self.S.partbytes[name] = part
        return t

    def ps(self, name, shape, dt=F32, part=None):
        t = self.stack.enter_context(self.nc.psum_tensor(name, shape, dt))
        if part is not None:
            self.S.partbytes[name] = part
        return t

    def din(self, name, shape, dt=F32):
        return self.nc.dram_tensor(name, list(shape), dt, kind="ExternalInput").ap()

    def dout(self, name, shape, dt=F32):
        return self.nc.dram_tensor(name, list(shape), dt, kind="ExternalOutput").ap()

    def dscr(self, name, shape, dt=F32):
        return self.nc.dram_tensor(name, list(shape), dt).ap()

    def cs(self, name, lo=0, hi=None):
        o, w = CST[name]
        if hi is None:
            hi = w
        return self.cst[:, o + lo:o + hi]

    def build(self):
        nc = self.nc
        T = self.T
        st = self.stack
        self.S = S = Sched(nc, st)
        self.x_d = self.din("x", [T, DM])
        self.p_d = self.din("pp", [DEPTH, T, DPLE])
        self.w_in_d = self.din("w_in", [DEPTH, DM, DP])
        self.w_out_d = self.din("w_out", [DEPTH, 2048, DM])
        self.w_gate_d = self.din("w_ple_gate", [DEPTH, DM, DM])
        self.w_pp_d = self.din("w_ple_proj", [DEPTH, DPLE, DM])
        self.ln_g_d = self.din("ln_g", [DEPTH, DM])
        self.ln_b_d = self.din("ln_b", [DEPTH, DM])
        self.ret_lr_d = self.din("ret_log_rate", [DEPTH, 8])
        self.gn_g_d = self.din("gn_g", [DEPTH, 2048])
        self.hg_lb_d = self.din("hgrn_lb_logits", [DEPTH, 2, 512])
        self.ml_b_d = self.din("ml_bias", [DEPTH, 16])
        self.conv_w_d = self.din("ssd_conv_w", [DEPTH, 5, 1024])
        self.conv_b_d = self.din("ssd_conv_b", [DEPTH, 1024])
        self.a_log_d = self.din("ssd_a_log", [DEPTH, 16])
        self.dt_b_d = self.din("ssd_dt_bias", [DEPTH, 16])
        self.ssd_d_d = self.din("ssd_d", [DEPTH, 8])
        self.cst_d = self.din("cst", [128, NCST])
        self.rope_d = self.din("rope", [T, 256])
        self.y_d = self.dout("y", [T, DM])
        self.h1_d = self.dscr("h1", [T, DM])
        self.wbf_d = self.dscr("wbf", [DEPTH, NPIECE, 128, 4096], BF16)
        self.stsc_d = self.dscr("stsc", [self.NCH, 128, NSTATE], BF16)

        self.P = [self.ps("P%d" % i, [128, 512], F32) for i in range(8)]
        self.pj_i = 0
        self.pj4_i = 0
        self.pj3_i = 0

        self.cst = self.sb("cst_sb", [128, NCST])
        self.identb = self.sb("identb", [128, 128], BF16)
        self.negfb = self.sb("negfb", [128, 128], BF16)
        self.negbb = self.sb("negbb", [128, 128], BF16)
        S.dma('sp', self.cst[:], self.cst_d)
        S.cp('dve', self.identb[:], self.cs('IDENT'))
        S.cp('dve', self.negfb[:], self.cs('NEGF'))
        S.cp('dve', self.negbb[:], self.cs('NEGB'))

        self.NW = 4
        self.WHOLD = 2
        self.wring = [self.sb("wr%d" % i, [128, 8, 512], BF16) for i in range(self.NW)]
        self.wsm = self.sb("wsm", [128, 8, 32], BF16)
        self.wpp = self.sb("wpp", [128, 2, 1024], BF16)
        self.wplan = []
        self.wplan_pos = 0
        self.wissued = 0

        self.g32 = [self.sb("g32_%d" % i, [128, 4, 512], F32, part=2048) for i in range(6)]
        big16 = self.sb("g16_01", [128, 4128], BF16, part=1024)
        self.g16 = [big16[:, 0:2048].rearrange("p (a c) -> p a c", a=4), big16[:, 2048:4096].rearrange("p (a c) -> p a c", a=4)]
        self.g16 += [self.sb("g16_%d" % i, [128, 4, 512], BF16, part=1024) for i in range(2, 7)]
        self.pre16 = big16[:, 0:4128].rearrange("p (t c) -> p t c", t=8)
        self.xT = self.sb("xT", [128, 8, 512], BF16, part=1024)
        self.mixed = self.sb("mixed", [128, 4, 2048], BF16, part=1024)
        self.gn_g = self.sb("gn_g_t", [128, 2048])
        self.ln_g = self.sb("ln_g_t", [128, DM])
        self.ln_b = self.sb("ln_b_t", [128, DM])
        self.small = self.sb("small", [128, 512])
        self.mc = self.sb("mc", [128, 4, 128])
        self.efb = self.sb("efb", [128, 2, 4, 128])
        self.Sf_ret = self.sb("S_ret", [128, 4, 128])
        self.Sb_ret = self.Sf_ret
        self.sh_ret = [self.sb("sh_ret%d" % i, [128, 512], BF16) for i in range(2)]
        self.Sf_ret16 = self.sh_ret[0][:].rearrange("p (h e) -> p h e", h=4)
        self.sbl = [self.sb("sbl%d" % i, [128, NSTATE], BF16, part=1024) for i in range(2)]
        self.tiny = self.sb("tiny", [128, 256])
        self.sm2 = self.sb("sm2", [128, 512])
        self.dg = self.sb("dg", [128, 5, 8, 128], BF16)
        self.S_ssd = self.sb("S_ssd", [128, 512])
        self.sh_ssd = [self.sb("sh_ssd%d" % i, [128, 512], BF16) for i in range(2)]
        self.xh32 = self.g32[5][0:4, 2:4, :].rearrange("p a c -> p (a c)")
        self.S_ml = self.sb("S_ml", [128, 4, 132])
        self.sh_ml = [self.sb("sh_ml%d" % i, [128, 528], BF16) for i in range(2)]
        self.S_hg = self.sb("S_hg", [128, 4, 128])
        self.sh_hg = [self.sb("sh_hg%d" % i, [128, 512], BF16) for i in range(2)]
        self.hgdec = self.sb("hgdec", [128, 64])
        self.hglb = self.sb("hglb", [128, 32])
        self.ones16 = self.sb("ones16", [128, 8], BF16)
        S.memset('dve', self.ones16[:], 1.0)
        for i in range(2):
            S.memset('dve', self.sh_ml[i][:], 0.0)
        self.hg_lb_setup()
        self.xTh = self.sb("xTh", [128, 8, 4], BF16)
        self.dbg_t = None

        if len(self.mixers) < 4:
            S.memset('pool', self.mixed[:], 0.0)
        self.convert_weights()
        for l in range(self.nlayers):
            self.layer_params(l)
            if 'ssd' in self.mixers:
                self.layer_params_ssd(l)
            self.plan_layer(l)
            src = self.x_d if l == 0 else self.h1_d
            dst = self.y_d if l == self.nlayers - 1 else self.h1_d
            self.zero_states_bwd()
            for blk in reversed(range(self.NBLK)):
                self.pre_block(l, blk, src)
            self.zero_states_fwd()
            for blk in range(self.NBLK):
                self.main_block(l, blk, src, dst)
        S.finish()
        print('sbuf bytes remaining', nc.sbuf_bytes_remaining, 'ops', S.nops, 'waits', S.nwaits)
        st.close()
        return nc

    def piece_src(self, l, pid):
        if pid < 17:
            cols = [RET0, RET0 + 512, RET0 + 1024, RET0 + 1536,
                    HG0, HG0 + 512, HG0 + 1024, HG0 + 1536, HG0 + 2048,
                    ML0, ML0 + 512, ML0 + 1024, ML0 + 1536, ML0 + 2048,
                    SS0, SS0 + 512, SS0 + 1024][pid]
            return self.w_in_d[l].rearrange("(k p) c -> p k c", p=128)[:, :, cols:cols + 512]
        if pid < 21:
            q = pid - 17
            kh, ch = q % 2, q // 2
            return self.w_out_d[l][kh * 1024:(kh + 1) * 1024, :].rearrange("(k p) c -> p k c", p=128)[:, :, ch * 512:(ch + 1) * 512]
        ch = pid - 21
        return self.w_gate_d[l].rearrange("(k p) c -> p k c", p=128)[:, :, ch * 512:(ch + 1) * 512]

    def convert_weights(self):
        S = self.S
        engs = ['dve', 'act', 'pool']
        n = 0
        for l in range(self.nlayers):
            for pid in range(NPIECE):
                stg = self.g32[2 * (n % 2)], self.g32[2 * (n % 2) + 1]
                src = self.piece_src(l, pid)
                S.dma('sp', stg[0][:], src[:, 0:4, :])
                S.dma('sp', stg[1][:], src[:, 4:8, :])
                ob = self.wring[n % self.NW]
                e = engs[n % 3]
                S.cp(e, ob[:, 0:4, :], stg[0][:])
                S.cp(engs[(n + 1) % 3], ob[:, 4:8, :], stg[1][:])
                S.dma('pool', self.wbf_d[l, pid].rearrange("p (k c) -> p k c", k=8), ob[:],
                      writes=[('wbf', l, pid)])
                n += 1

    def plan_layer(self, l):
        pre, main = [], []
        if 'ret' in self.mixers:
            pre += [1, 2]
            main += [0, 1, 2, 3]
        if 'hg' in self.mixers:
            pre += [6, 7]
            main += [4, 5, 6, 7, 8]
        if 'ml' in self.mixers:
            pre += [10, 11]
            main += [9, 10, 11, 12, 13]
        if 'ssd' in self.mixers:
            pre += [15, 16]
            main += [14, 15, 16]
        tail = [17, 18, 19, 20, 21, 22]
        for blk in range(self.NBLK):
            for pid in pre:
                self.wplan.append((l, pid))
        for blk in range(self.NBLK):
            for pid in main + tail:
                self.wplan.append((l, pid))

    def _issue_piece(self):
        if self.wissued >= len(self.wplan):
            return
        l, pid = self.wplan[self.wissued]
        slot = self.wring[self.wissued % self.NW]
        self.S.dma('sp', slot[:], self.wbf_d[l, pid].rearrange("p (k c) -> p k c", k=8),
                   reads=[('wbf', l, pid)])
        self.wissued += 1

    def next_piece(self, l, pid):
        assert self.wplan[self.wplan_pos] == (l, pid), (self.wplan[self.wplan_pos], l, pid)
        while self.wissued < min(len(self.wplan), max(self.wplan_pos + 1, self.wplan_pos - self.WHOLD + 1 + self.NW)):
            self._issue_piece()
        slot = self.wring[self.wplan_pos % self.NW]
        self.wplan_pos += 1
        return slot

    def layer_params(self, l):
        S = self.S
        S.dma('pool', self.gn_g[:], self.gn_g_d[l].partition_broadcast(128))
        S.dma('pool', self.ln_g[:], self.ln_g_d[l].partition_broadcast(128))
        S.dma('pool', self.ln_b[:], self.ln_b_d[l].partition_broadcast(128))
        wv = self.w_in_d[l].rearrange("(k p) c -> p k c", p=128)
        v = self.g32[2][:, 0, 0:256].rearrange("p (k c) -> p k c", k=8)
        S.dma('pool', v[:, :, 0:16], wv[:, :, ML0 + 2560:ML0 + 2576])
        S.dma('pool', v[:, :, 16:32], wv[:, :, SS0 + 1536:SS0 + 1552])
        S.cp('dve', self.wsm[:], v)
        v2 = self.g32[3][:].rearrange("p a c -> p (a c)").rearrange("p (k c) -> p k c", k=2)
        S.dma('pool', v2, self.w_pp_d[l].rearrange("(k p) c -> p k c", p=128))
        S.cp('dve', self.wpp[:], v2)
        S.dma('pool', self.small[:, 160:176], self.ml_b_d[l].partition_broadcast(128))
        sm = self.small
        S.dma('pool', sm[:, 0:8], self.ret_lr_d[l].partition_broadcast(128))
        S.act(sm[:, 16:24], sm[:, 0:8], AF.Exp)
        S.ts('dve', sm[:, 8:16], sm[:, 16:24], -1.0, None, ALU.mult)
        cols = self.cs('COLS')
        for h in range(4):
            lgf = sm[:, 8 + h:9 + h]
            lgb = sm[:, 12 + h:13 + h]
            S.act(sm[:, 32 + h:33 + h], cols[:, 0:1], AF.Exp, scale=lgf)
            S.act(sm[:, 36 + h:37 + h], cols[:, 1:2], AF.Exp, scale=lgb)
            S.act(sm[:, 40 + h:41 + h], cols[:, 2:3], AF.Exp, scale=lgf)
            S.act(sm[:, 44 + h:45 + h], cols[:, 3:4], AF.Exp, scale=lgb)
            S.act(self.efb[:, 0, h, :], self.cs('ROWF'), AF.Exp, scale=lgf)
            S.act(self.efb[:, 1, h, :], self.cs('ROWB'), AF.Exp, scale=lgb)
            S.act(sm[:, 48 + h:49 + h], lgf, AF.Exp, scale=float(C))
            S.act(sm[:, 52 + h:53 + h], lgb, AF.Exp, scale=float(C))
            t1 = self.g32[0][:, 0, 0:128]
            t2 = self.g32[0][:, 0, 128:256]
            S.act(t1, self.cs('DIFF'), AF.Exp, scale=lgf)
            S.tt('dve', t1, t1, self.cs('LOW'), ALU.mult)
            S.act(t2, self.cs('DIFF'), AF.Exp, scale=sm[:, 20 + h:21 + h])
            S.tt('dve', t2, t2, self.cs('UP'), ALU.mult)
            S.tt('dve', self.mc[:, h, :], t1, t2, ALU.add)

    def layer_params_ssd(self, l):
        S = self.S
        sm = self.small
        S.dma('pool', sm[:, 64:80], self.a_log_d[l].partition_broadcast(128))
        S.act(sm[:, 64:80], sm[:, 64:80], AF.Exp)
        S.ts('dve', sm[:, 64:80], sm[:, 64:80], -1.0, None, ALU.mult)
        S.dma('pool', sm[:, 80:96], self.dt_b_d[l].partition_broadcast(128))
        S.dma('pool', sm[:, 96:104], self.ssd_d_d[l].partition_broadcast(128))
        S.dma('pool', sm[:, 104:112], self.conv_b_d[l].rearrange("(t p) -> p t", p=128), allow_slow_non_contiguous=True)
        cw = sm[:, 112:152].rearrange("p (k t) -> p k t", k=5)
        for k in range(5):
            S.dma('pool', cw[:, k, :], self.conv_w_d[l][k].rearrange("(t p) -> p t", p=128), allow_slow_non_contiguous=True)
        for k in range(5):
            for t in range(8):
                S.ts('dve' if (k + t) % 2 == 0 else 'pool', self.dg[:, k, t, :], self.cs('IDENT'), cw[:, k, t:t + 1], None, ALU.mult)

    def zero_states_bwd(self):
        self.S.memset('dve', self.Sb_ret[:], 0.0)
        self.S.memset('dve', self.S_ssd[:], 0.0)
        self.S.memset('dve', self.S_ml[:], 0.0)
        self.S.memset('dve', self.S_hg[:], 0.0)

    def zero_states_fwd(self):
        self.S.memset('dve', self.Sf_ret[:], 0.0)
        self.S.memset('dve', self.Sf_ret16, 0.0)
        self.S.memset('dve', self.S_ssd[:], 0.0)
        self.S.memset('dve', self.sh_ssd[0][:], 0.0)
        self.S.memset('dve', self.S_ml[:], 0.0)
        self.S.memset('dve', self.sh_ml[0][:], 0.0)
        self.S.memset('dve', self.S_hg[:], 0.0)
        self.S.memset('dve', self.sh_hg[0][:], 0.0)

    def xs32(self, s):
        return self.g32[2 + s // 2][:, 2 * (s % 2):2 * (s % 2) + 2, :].rearrange("p a c -> p (a c)")

    def load_x32(self, blk, src, l):
        for hf in range(2):
            self.S.dma('pool', self.g32[2 + hf][:].rearrange("p (s a) c -> p s (a c)", s=2),
                       src[blk * TB + hf * 256:blk * TB + (hf + 1) * 256, :].rearrange("(s p) d -> p s d", p=128),
                       reads=[('hx', l, blk * NSUB + 2 * hf + s) for s in range(2)])

    def load_xT(self, blk, src):
        S = self.S
        if getattr(self, 'x_prefetched', None) == (self.cur_layer, blk, self.in_main):
            self.x_prefetched = None
        else:
            self.load_x32(blk, src, self.cur_layer)
        ident = self.cs('IDENT')
        for k in range(8):
            bank = self.P[2 + (k % 2)]
            for s in range(NSUB):
                S.tr(bank[:, s * 128:(s + 1) * 128], self.xs32(s)[:, k * 128:(k + 1) * 128], ident)
            S.cp('act' if k % 2 == 0 else 'dve', self.xT[:, k, :], bank[:, :])

    def load_rope(self, blk):
        t = self.g32[5][:, :, 0:256]
        self.S.dma('pool', t, self.rope_d[blk * TB:(blk + 1) * TB, :].rearrange("(s p) c -> p s c", p=128))
        return t

    def proj_tm(self, W, ncols, s, kt=8, xT=None):
        S = self.S
        xT = self.xT if xT is None else xT
        bank = [self.P[0], self.P[1], self.P[5], self.P[6]][self.pj4_i % 4]
        self.pj4_i += 1
        for k in range(kt):
            S.mm(bank[:, 0:ncols], xT[:, k, s * 128:(s + 1) * 128], W[:, k, 0:ncols], start=(k == 0), stop=(k == kt - 1))
        return bank[:, 0:ncols]

    def proj_fm(self, W, c, kt=8):
        S = self.S
        bank = [self.P[0], self.P[1], self.P[5], self.P[6]][self.pj4_i % 4]
        self.pj4_i += 1
        for k in range(kt):
            S.mm(bank[:, :], W[:, k, c * 128:(c + 1) * 128], self.xT[:, k, :], start=(k == 0), stop=(k == kt - 1))
        return bank[:, :]

    def rope(self, dst16, src32, ropet, koff):
        S = self.S
        v = src32[:].rearrange("p s (h t f) -> p s h t f", h=4, t=2)
        o = dst16[:].rearrange("p s (h t f) -> p s h t f", h=4, t=2)
        x1, x2 = v[:, :, :, 0, :], v[:, :, :, 1, :]
        o1, o2 = o[:, :, :, 0, :], o[:, :, :, 1, :]
        cos = ropet[:, :, koff:koff + 64].unsqueeze(2).to_broadcast([128, 4, 4, 64])
        sin = ropet[:, :, koff + 64:koff + 128].unsqueeze(2).to_broadcast([128, 4, 4, 64])
        ta = self.g32[4][:, 0:2, :].rearrange("p a (h f) -> p (a h) f", f=64).rearrange("p (s h) f -> p s h f", s=4)
        tb = self.g32[4][:, 2:4, :].rearrange("p a (h f) -> p (a h) f", f=64).rearrange("p (s h) f -> p s h f", s=4)
        S.tt('dve', ta, x1, cos, ALU.mult)
        S.tt('dve', tb, x2, sin, ALU.mult)
        S.tt('dve', o1, ta, tb, ALU.subtract)
        S.tt('dve', ta, x2, cos, ALU.mult)
        S.tt('dve', tb, x1, sin, ALU.mult)
        S.tt('dve', o2, ta, tb, ALU.add)

    def transpose_heads(self, dstT, src16, nh=4):
        S = self.S
        views = [self.P[2][:].bitcast(BF16), self.P[3][:].bitcast(BF16)]
        for h in range(nh):
            pt = views[h % 2]
            for s in range(NSUB):
                S.tr(pt[:, s * 128:(s + 1) * 128], src16[:, s, h * 128:(h + 1) * 128], self.identb[:])
            S.cp('act' if h % 2 == 0 else 'dve', dstT[:, h, :], pt[:, 0:512])

    def group_norm(self, o32, G, n, center, gcol, gate, out, tbase):
        S = self.S
        tn = self.tiny
        s1 = tn[:, tbase:tbase + G]
        ss = tn[:, tbase + 8:tbase + 8 + G]
        mean = tn[:, tbase + 16:tbase + 16 + G]
        var = tn[:, tbase + 24:tbase + 24 + G]
        rstd = tn[:, tbase + 32:tbase + 32 + G]
        nmr = tn[:, tbase + 40:tbase + 40 + G]
        o3 = o32.rearrange("p (g n) -> p g n", g=G)
        junk = self.g32[4][:, 0, 0:n]
        for g in range(G):
            S.act(junk, o3[:, g, :], AF.Square, accum=ss[:, g:g + 1])
        if center:
            S.rsum('dve', s1, o3)
            S.ts('dve', mean, s1, 1.0 / n, None, ALU.mult)
            S.tt('dve', var, mean, mean, ALU.mult)
            S.stt('dve', var, ss, 1.0 / n, var, ALU.mult, ALU.subtract)
        else:
            S.ts('dve', var, ss, 1.0 / n, None, ALU.mult)
        S.ts('dve', var, var, EPS, None, ALU.add)
        S.act(rstd, var, AF.Sqrt)
        S.recip(rstd, rstd)
        if center:
            S.stt('dve', nmr, mean, -1.0, rstd, ALU.mult, ALU.mult)
        for g in range(G):
            if center:
                S.act(o3[:, g, :], o3[:, g, :], AF.Identity, bias=nmr[:, g:g + 1], scale=rstd[:, g:g + 1])
            else:
                S.act(o3[:, g, :], o3[:, g, :], AF.Identity, scale=rstd[:, g:g + 1])
        if gate is not None:
            S.tt('pool', o32, o32, self.gn_g[:, gcol:gcol + G * n], ALU.mult)
            S.tt('dve', out, o32, gate, ALU.mult)
        else:
            S.tt('dve', out, o32, self.gn_g[:, gcol:gcol + G * n], ALU.mult)


    def group_norm2(self, src, o32, G, n, center, mul_ap, out, tb, junk=None):
        S = self.S
        tn = self.tiny
        ss, s1, mean, var, rstd = (tn[:, tb + 8 * i:tb + 8 * i + G] for i in range(5))
        s3 = src.rearrange("p (g n) -> p g n", g=G)
        o3 = o32.rearrange("p (g n) -> p g n", g=G)
        for g in range(G):
            S.act((self.g32[4] if junk is None else junk)[:, g % 4, 0:n], s3[:, g, :], AF.Square, accum=ss[:, g:g + 1])
        epsc = self.cs('COLS', 4, 5)
        if center:
            S.rsum('dve', s1, s3)
            S.ts('dve', mean, s1, 1.0 / n, None, ALU.mult)
            S.tt('dve', var, mean, mean, ALU.mult)
            S.stt('dve', var, ss, 1.0 / n, var, ALU.mult, ALU.subtract)
            S.act(rstd, var, AF.Ln, bias=epsc)
        else:
            S.act(rstd, ss, AF.Ln, bias=epsc, scale=1.0 / n)
        S.act(rstd, rstd, AF.Exp, scale=-0.5)
        for g in range(G):
            if center:
                S.ts('dve', o3[:, g, :], s3[:, g, :], mean[:, g:g + 1], rstd[:, g:g + 1], ALU.subtract, ALU.mult)
            else:
                S.ts('dve', o3[:, g, :], s3[:, g, :], rstd[:, g:g + 1], None, ALU.mult)
        S.tt('dve', out, o32, mul_ap, ALU.mult)

    def premul_gate(self, sg, gcol):
        self.S.tt('pool', sg[:], sg[:], self.gn_g[:, gcol:gcol + 512].unsqueeze(1).to_broadcast([128, 4, 512]), ALU.mult)

    def ret_project(self, l, blk, main):
        S = self.S
        ropet = self.load_rope(blk)
        q32, k32, sg = self.g32[0], self.g32[1], self.g32[2]
        q16, k16, v16 = self.g16[0], self.g16[1], self.g16[2]
        if main:
            W = self.next_piece(l, 0)
            for s in range(NSUB):
                S.cp('act', q32[:, s, :], self.proj_tm(W, 512, s))
        W = self.next_piece(l, 1)
        for s in range(NSUB):
            S.cp('act', k32[:, s, :], self.proj_tm(W, 512, s))
        if main:
            self.rope(q16, q32, ropet, 0)
        W = self.next_piece(l, 2)
        for s in range(NSUB):
            S.cp('act', v16[:, s, :], self.proj_tm(W, 512, s))
        self.rope(k16, k32, ropet, 128)
        if main:
            W = self.next_piece(l, 3)
            for s in range(NSUB):
                S.act(sg[:, s, :], self.proj_tm(W, 512, s), AF.Silu)
            self.premul_gate(sg, 0)
        return q16, k16, v16, sg

    def ret_state_update(self, state, kin16, v16, s, dec, pu=None):
        S = self.S
        pu = self.P[7] if pu is None else pu
        for h in range(4):
            S.mm(pu[:, h * 128:(h + 1) * 128], kin16[:, s, h * 128:(h + 1) * 128], v16[:, s, h * 128:(h + 1) * 128])
        for h in range(4):
            S.stt('dve', state[:, h, :], state[:, h, :], dec[:, h:h + 1], pu[:, h * 128:(h + 1) * 128], ALU.mult, ALU.add)

    def ret_pre(self, l, blk):
        S = self.S
        sm = self.small
        _, k16, v16, _ = self.ret_project(l, blk, False)
        kin = self.g16[3]
        wkb = sm[:, 44:48].unsqueeze(1).unsqueeze(3).to_broadcast([128, 4, 4, 128])
        S.tt('dve', kin[:].rearrange("p s (h d) -> p s h d", h=4), k16[:].rearrange("p s (h d) -> p s h d", h=4), wkb, ALU.mult)
        for s in reversed(range(NSUB)):
            n = blk * NSUB + s
            sh = self.sh_ret[n % 2]
            S.cp('act', sh[:], self.Sb_ret[:].rearrange("p h d -> p (h d)"))
            S.dma('pool', self.stsc_d[n][:, ST_RET:ST_RET + 512], sh[:], writes=[('stsc', n, 'ret')])
            self.ret_state_update(self.Sb_ret, kin, v16, s, sm[:, 52:56], pu=self.P[7 - s % 2])

    def ret_main(self, l, blk):
        S = self.S
        sm = self.small
        q16, k16, v16, sgg = self.ret_project(l, blk, True)
        qT, kT, kin = self.g16[3], self.g16[4], self.g16[5]
        qfT, qbT = self.g16[0], self.g16[1]
        wkf = sm[:, 40:44].unsqueeze(1).unsqueeze(3).to_broadcast([128, 4, 4, 128])
        S.tt('pool', kin[:].rearrange("p s (h d) -> p s h d", h=4), k16[:].rearrange("p s (h d) -> p s h d", h=4), wkf, ALU.mult)
        self.transpose_heads(kT, k16)
        views = [self.P[2][:].bitcast(BF16), self.P[3][:].bitcast(BF16)]
        for h in range(4):
            pv = views[h // 2]
            half = (h % 2) * 512
            for s in range(NSUB):
                S.tr(pv[:, half + s * 128:half + (s + 1) * 128], q16[:, s, h * 128:(h + 1) * 128], self.identb[:])
        v4 = lambda ap: ap.rearrange("p (s i) -> p s i", s=4)
        for h in range(4):
            src = views[h // 2][:, (h % 2) * 512:(h % 2) * 512 + 512]
            S.cp('act', qT[:, h, :], src)
            S.tt('dve', v4(qfT[:, h, :]), v4(qT[:, h, :]), self.efb[:, 0, h, :].unsqueeze(1).to_broadcast([128, 4, 128]), ALU.mult)
            S.tt('dve', v4(qbT[:, h, :]), v4(qT[:, h, :]), self.efb[:, 1, h, :].unsqueeze(1).to_broadcast([128, 4, 128]), ALU.mult)
        o32 = self.g32[3]
        pending = None
        for s in range(NSUB):
            n = blk * NSUB + s
            sbl = self.get_sbl(n, 'ret')
            sl = slice(s * 128, (s + 1) * 128)
            shr = self.sh_ret[s % 2][:].rearrange("p (h e) -> p h e", h=4)
            shw = self.sh_ret[(s + 1) % 2][:].rearrange("p (h e) -> p h e", h=4)
            self.ret_state_update(self.Sf_ret, kin, v16, s, sm[:, 48:52], pu=self.P[7] if s % 2 == 0 else self.P[0])
            S.cp('act', shw, self.Sf_ret[:])
            for h in range(4):
                S.mm(self.P[6][:, h * 128:(h + 1) * 128], kT[:, h, sl], qT[:, h, sl])
            pt16 = self.g16[6][:, s % 2, :]
            S.tt('dve', pt16.rearrange("p (h i) -> p h i", h=4), self.P[6][:].rearrange("p (h i) -> p h i", h=4), self.mc[:], ALU.mult)
            po = self.P[4 + s % 2]
            for h in range(4):
                hs = slice(h * 128, (h + 1) * 128)
                S.mm(po[:, hs], pt16[:, hs], v16[:, s, hs], start=True, stop=False)
                S.mm(po[:, hs], qfT[:, h, sl], shr[:, h, :], start=False, stop=False)
                S.mm(po[:, hs], qbT[:, h, sl], sbl[:, ST_RET + h * 128:ST_RET + (h + 1) * 128], start=False, stop=True)
            if pending is not None:
                pending()
            S.cp('act', o32[:, s, :], po[:, :])
            pending = (lambda s=s: self.group_norm2(o32[:, s, :], o32[:, s, :], 4, 128, True, sgg[:, s, :], self.mixed[:, s, 0:512], (s % 2) * 48))
        pending()

    def gates_project(self, l, blk):
        S = self.S
        gts = self.sm2[:, 0:128].rearrange("p (s c) -> p s c", s=4)
        for s in range(NSUB):
            S.cp('act', gts[:, s, :], self.proj_tm(self.wsm, 32, s))
        return gts

    def decay_prep_all(self, lf, lw, H, dirs):
        S = self.S
        H2 = 2 * H
        arr = lambda i: self.sm2[:, 128 + 64 * i:128 + 64 * i + 4 * H2].rearrange("p (s c) -> p s c", s=4)
        cum, tot, bias, dec = arr(0), arr(1), arr(2), arr(3)
        pv = self.P[2][:, 0:8 * H2].rearrange("p (s c) -> p s c", s=4)
        for s in range(NSUB):
            if 'f' in dirs:
                S.mm(pv[:, s, 0:H], self.cs('LOW'), lf[:, s, 0:H])
            if 'b' in dirs:
                S.mm(pv[:, s, H:H2], self.cs('UP'), lf[:, s, H:H2])
            S.mm(pv[:, s, H2:2 * H2], self.cs('ONES'), lf[:, s, 0:H2])
        lo, hi = (0 if 'f' in dirs else H), (H2 if 'b' in dirs else H)
        S.cp('act', cum[:, :, lo:hi], pv[:, :, lo:hi])
        S.cp('act', tot, pv[:, :, H2:2 * H2])
        S.tt('dve', bias[:, :, lo:hi], lw[:, :, lo:hi], cum[:, :, lo:hi], ALU.subtract)
        S.act(dec[:, :, lo:hi], tot[:, :, lo:hi], AF.Exp)
        S.tt('dve', tot[:, :, lo:hi], tot[:, :, lo:hi], bias[:, :, lo:hi], ALU.add)
        S.act(tot[:, :, lo:hi], tot[:, :, lo:hi], AF.Exp)
        S.act(cum[:, :, lo:hi], cum[:, :, lo:hi], AF.Exp)
        return dict(bias=bias, eq=cum, wk=tot, dec=dec)

    def decay_matrix(self, region, lfcol, bias_col, fwd, out):
        S = self.S
        S.mm(region, lfcol.to_broadcast([128, 128]), self.cs('LOW') if fwd else self.cs('UP'), start=True, stop=False)
        S.mm(region, self.identb[:], self.negfb[:] if fwd else self.negbb[:], start=False, stop=True)
        S.act(out, region, AF.Exp, bias=bias_col)

    def ssd_project(self, l, blk, src, main):
        S = self.S
        sm = self.small
        T = self.T
        ident = self.cs('IDENT')
        sz = self.g32[0]
        if main:
            W = self.next_piece(l, 14)
            for s in range(NSUB):
                S.act(sz[:, s, :], self.proj_tm(W, 512, s), AF.Silu)
        S.memset('dve', self.xh32, 0.0)
        t0 = blk * TB
        if blk > 0:
            S.dma('pool', self.g32[5][0:2, 2:4, :].rearrange("p a c -> p (a c)"), src[t0 - 2:t0, :], reads=[('hx', l, blk * NSUB - 1)])
        if blk < self.NBLK - 1:
            S.dma('pool', self.g32[5][2:4, 2:4, :].rearrange("p a c -> p (a c)"), src[t0 + TB:t0 + TB + 2, :], reads=[('hx', l, (blk + 1) * NSUB)])
        ph = self.P[3]
        for k in range(8):
            S.tr(ph[:, k * 4:(k + 1) * 4], self.xh32[:, k * 128:(k + 1) * 128], ident[0:4, 0:4])
        S.cp('act', self.xTh[:].rearrange("p k c -> p (k c)"), ph[:, 0:32])
        pre = self.pre16
        ph2 = self.P[4]
        for half in range(2):
            W = self.next_piece(l, 15 + half)
            for c in range(4):
                t = half * 4 + c
                S.cp('act', pre[:, t, 2:514], self.proj_fm(W, c))
                for k in range(8):
                    S.mm(ph2[:, t * 4:(t + 1) * 4], W[:, k, c * 128:(c + 1) * 128], self.xTh[:, k, :], start=(k == 0), stop=(k == 7))
        phv = ph2[:, 0:32].rearrange("p (t c) -> p t c", c=4)
        S.cp('dve', pre[:, :, 0:2], phv[:, :, 0:2])
        S.cp('dve', pre[:, :, 514:516], phv[:, :, 2:4])
        xsT, BCT = self.g16[2], self.g16[3]
        for t in range(8):
            if not main and t >= 6:
                break
            bank = [self.P[0], self.P[1], self.P[5], self.P[6]][self.pj4_i % 4]
            self.pj4_i += 1
            for k in range(5):
                S.mm(bank[:, :], self.dg[:, k, t, :], pre[:, t, k:k + 512], start=(k == 0), stop=(k == 4))
            dst = xsT[:, t, :] if t < 4 else BCT[:, t - 4, :]
            S.act(dst, bank[:, :], AF.Silu, bias=sm[:, 104 + t:105 + t])
        xs16, B16 = self.g16[4], self.g16[5]
        views = [self.P[2][:].bitcast(BF16), self.P[3][:].bitcast(BF16)]
        for s in range(NSUB):
            pt = views[s % 2]
            for t in range(4):
                S.tr(pt[:, t * 128:(t + 1) * 128], xsT[:, t, s * 128:(s + 1) * 128], self.identb[:])
            S.cp('act' if s % 2 == 0 else 'dve', xs16[:, s, :], pt[:, 0:512])
        for s in range(NSUB):
            pt = views[s % 2]
            for g in range(2):
                S.tr(pt[:, g * 128:(g + 1) * 128], BCT[:, g, s * 128:(s + 1) * 128], self.identb[:])
            S.cp('dve' if s % 2 == 0 else 'act', B16[:, s, 0:256], pt[:, 0:256])
        gts = self.sm2[:, 0:128].rearrange("p (s c) -> p s c", s=4)
        dt = self.sm2[:, 384:448].rearrange("p (s c) -> p s c", s=4)
        lf = self.sm2[:, 448:512].rearrange("p (s c) -> p s c", s=4)
        ldt = self.tiny[:, 128:192].rearrange("p (s c) -> p s c", s=4)
        one = self.cs('ONES', 0, 1)
        S.tt('dve', dt, gts[:, :, 16:32], sm[:, 80:96].unsqueeze(1).to_broadcast([128, 4, 16]), ALU.add)
        S.act(dt, dt, AF.Exp)
        S.act(dt, dt, AF.Ln, bias=one)
        S.act(ldt, dt, AF.Ln)
        S.tt('dve', lf, dt, sm[:, 64:80].unsqueeze(1).to_broadcast([128, 4, 16]), ALU.mult)
        return dict(sz=sz, xs16=xs16, B16=B16, BT=BCT[:, 0:2, :], CT=BCT[:, 2:4, :], lf=lf, ldt=ldt)

    def ssd_state_update(self, t, s, vs16, dec, pu=None):
        S = self.S
        pu = self.P[7] if pu is None else pu
        for g in range(2):
            S.mm(pu[:, g * 256:(g + 1) * 256], t['B16'][:, s, g * 128:(g + 1) * 128], vs16[:, g * 256:(g + 1) * 256])
        S3 = self.S_ssd[:].rearrange("p (h e) -> p h e", h=8)
        S.tt('dve', S3, S3, dec.unsqueeze(2).to_broadcast([128, 8, 64]), ALU.mult)
        S.tt('dve', self.S_ssd[:], self.S_ssd[:], pu[:, :], ALU.add)

    def ssd_vs(self, t, wk, dst):
        self.S.tt('dve', dst[:].rearrange("p s (h e) -> p s h e", h=8), t['xs16'][:].rearrange("p s (h e) -> p s h e", h=8),
                  wk.unsqueeze(3).to_broadcast([128, 4, 8, 64]), ALU.mult)

    def ssd_pre(self, l, blk, src):
        S = self.S
        t = self.ssd_project(l, blk, src, False)
        d = self.decay_prep_all(t['lf'], t['ldt'], 8, 'b')
        vs = self.g16[6]
        self.ssd_vs(t, d['wk'][:, :, 8:16], vs)
        for s in reversed(range(NSUB)):
            n = blk * NSUB + s
            sh = self.sh_ssd[n % 2]
            S.cp('act', sh[:], self.S_ssd[:])
            S.dma('pool', self.stsc_d[n][:, ST_SSD:ST_SSD + 512], sh[:], writes=[('stsc', n, 'ssd')])
            self.ssd_state_update(t, s, vs[:, s, :], d['dec'][:, s, 8:16], pu=self.P[7 - s % 2])

    def ssd_main(self, l, blk, src):
        S = self.S
        sm = self.small
        t = self.ssd_project(l, blk, src, True)
        d = self.decay_prep_all(t['lf'], t['ldt'], 8, 'fb')
        vs = self.g16[0]
        self.ssd_vs(t, d['wk'][:, :, 0:8], vs)
        xd = self.g32[2]
        S.tt('pool', xd[:].rearrange("p s (h e) -> p s h e", h=8), t['xs16'][:].rearrange("p s (h e) -> p s h e", h=8),
             sm[:, 96:104].unsqueeze(1).unsqueeze(3).to_broadcast([128, 4, 8, 64]), ALU.mult)
        E = self.g32[1]
        Ef = E[:, 0:2, :].rearrange("p a (h i) -> p (a h) i", i=128)
        Eb = E[:, 2:4, :].rearrange("p a (h i) -> p (a h) i", i=128)
        y32t, t32 = self.g32[3], self.g32[4][:, 3, :]
        v8 = lambda ap: ap.rearrange("p (h e) -> p h e", h=8)
        pending = None
        for s in range(NSUB):
            n = blk * NSUB + s
            sbl = self.get_sbl(n, 'ssd')
            sl = slice(s * 128, (s + 1) * 128)
            shr, shw = self.sh_ssd[s % 2], self.sh_ssd[(s + 1) % 2]
            self.ssd_state_update(t, s, vs[:, s, :], d['dec'][:, s, 0:8])
            S.cp('act', shw[:], self.S_ssd[:])
            for g in range(2):
                S.mm(self.P[3][:, g * 128:(g + 1) * 128], t['BT'][:, g, sl], t['CT'][:, g, sl])
            qbanks = [self.P[4], self.P[5], self.P[0], self.P[1]]
            for q in range(4):
                dirn, hb = q // 2, (q % 2) * 4
                for hh in range(4):
                    c = dirn * 8 + hb + hh
                    reg = qbanks[q][:, hh * 128:(hh + 1) * 128]
                    S.mm(reg, t['lf'][:, s, c:c + 1].to_broadcast([128, 128]), self.cs('LOW') if dirn == 0 else self.cs('UP'), start=True, stop=False)
                    S.mm(reg, self.identb[:], self.negfb[:] if dirn == 0 else self.negbb[:], start=False, stop=True)
            for q in range(4):
                dirn, hb = q // 2, (q % 2) * 4
                for hh in range(4):
                    c = dirn * 8 + hb + hh
                    S.act((Ef if dirn == 0 else Eb)[:, hb + hh, :], qbanks[q][:, hh * 128:(hh + 1) * 128], AF.Exp, bias=d['bias'][:, s, c:c + 1])
            S.tt('dve', E[:, 0:2, :], E[:, 0:2, :], E[:, 2:4, :], ALU.add)
            pt16 = self.g16[6][:, 2 * (s % 2):2 * (s % 2) + 2, :].rearrange("p a (h i) -> p (a h) i", i=128)
            S.tt('dve', pt16.rearrange("p (g h) i -> p g h i", g=2),
                 self.P[3][:, 0:256].rearrange("p (g i) -> p g i", g=2).unsqueeze(2).to_broadcast([128, 2, 4, 128]),
                 Ef.rearrange("p (g h) i -> p g h i", g=2), ALU.mult)
            for h in range(8):
                S.mm(self.P[6][:, h * 64:(h + 1) * 64], pt16[:, h, :], t['xs16'][:, s, h * 64:(h + 1) * 64])
            for g in range(2):
                S.mm(self.P[2][:, g * 256:(g + 1) * 256], t['CT'][:, g, sl], shr[:, g * 256:(g + 1) * 256])
            for g in range(2):
                S.mm(self.P[7][:, g * 256:(g + 1) * 256], t['CT'][:, g, sl], sbl[:, ST_SSD + g * 256:ST_SSD + (g + 1) * 256])
            y32 = y32t[:, s, :]
            S.tt('dve', v8(y32), v8(self.P[2][:, :]), d['eq'][:, s, 0:8].unsqueeze(2).to_broadcast([128, 8, 64]), ALU.mult)
            S.tt('dve', v8(t32), v8(self.P[7][:, :]), d['eq'][:, s, 8:16].unsqueeze(2).to_broadcast([128, 8, 64]), ALU.mult)
            S.tt('dve', t32, t32, xd[:, s, :], ALU.add)
            S.tt('dve', y32, y32, self.P[6][:, :], ALU.add)
            S.tt('dve', y32, y32, t32, ALU.add)
            if pending is not None:
                pending()

            def tail(s=s, y32=y32):
                S.tt('dve', y32, y32, t['sz'][:, s, :], ALU.mult)
                self.group_norm2(y32, y32, 2, 256, False, self.gn_g[:, 1536:2048], self.mixed[:, s, 1536:2048], (s % 2) * 48)
            pending = tail
        pending()

    def ml_project(self, l, blk, main):
        S = self.S
        sm = self.small
        q16, k16, v16 = self.g16[0], self.g16[1], self.g16[2]
        so, sg = self.g32[0], self.g32[1]
        if main:
            W = self.next_piece(l, 9)
            for s in range(NSUB):
                S.cp('act', q16[:, s, :], self.proj_tm(W, 512, s))
        W = self.next_piece(l, 10)
        for s in range(NSUB):
            S.act(k16[:, s, :], self.proj_tm(W, 512, s), AF.Identity, scale=float(128 ** -0.5))
        W = self.next_piece(l, 11)
        for s in range(NSUB):
            S.cp('act', v16[:, s, :], self.proj_tm(W, 512, s))
        if main:
            W = self.next_piece(l, 12)
            for s in range(NSUB):
                S.act(so[:, s, :], self.proj_tm(W, 512, s), AF.Sigmoid)
            W = self.next_piece(l, 13)
            for s in range(NSUB):
                S.act(sg[:, s, :], self.proj_tm(W, 512, s), AF.Silu)
        gts = self.sm2[:, 0:128].rearrange("p (s c) -> p s c", s=4)
        ip = self.tiny[:, 192:224].rearrange("p (s c) -> p s c", s=4)
        lf = self.tiny[:, 224:256].rearrange("p (s c) -> p s c", s=4)
        one = self.cs('ONES', 0, 1)
        S.tt('dve', ip, gts[:, :, 0:8], sm[:, 160:168].unsqueeze(1).to_broadcast([128, 4, 8]), ALU.add)
        S.tt('dve', lf, gts[:, :, 8:16], sm[:, 168:176].unsqueeze(1).to_broadcast([128, 4, 8]), ALU.add)
        S.act(lf, lf, AF.Exp, scale=-1.0)
        S.act(lf, lf, AF.Ln, bias=one)
        S.ts('dve', lf, lf, -1.0, None, ALU.mult)
        return dict(q16=q16, k16=k16, v16=v16, so=so, sg=sg, ip=ip, lf=lf)

    def ml_kin(self, t, wk):
        kin = self.g16[5]
        self.S.tt('dve', kin[:].rearrange("p s (h d) -> p s h d", h=4), t['k16'][:].rearrange("p s (h d) -> p s h d", h=4),
                  wk.unsqueeze(3).to_broadcast([128, 4, 4, 128]), ALU.mult)
        return kin

    def ml_state_update(self, t, kin, s, dec, pu=None):
        S = self.S
        pu = self.P[7] if pu is None else pu
        for grp in ((0, 1, 2), (3,)):
            for h in grp:
                r = (h % 3) * 129
                hs = slice(h * 128, (h + 1) * 128)
                S.mm(pu[:, r:r + 128], kin[:, s, hs], t['v16'][:, s, hs])
                S.mm(pu[:, r + 128:r + 129], kin[:, s, hs], self.ones16[:, 0:1])
            for h in grp:
                r = (h % 3) * 129
                S.stt('dve', self.S_ml[:, h, 0:129], self.S_ml[:, h, 0:129], dec[:, h:h + 1], pu[:, r:r + 129], ALU.mult, ALU.add)

    def ml_pre(self, l, blk):
        S = self.S
        t = self.ml_project(l, blk, False)
        d = self.decay_prep_all(t['lf'], t['ip'], 4, 'b')
        kin = self.ml_kin(t, d['wk'][:, :, 4:8])
        for s in reversed(range(NSUB)):
            n = blk * NSUB + s
            sh = self.sh_ml[n % 2]
            S.cp('act', sh[:].rearrange("p (h e) -> p h e", h=4)[:, :, 0:129], self.S_ml[:, :, 0:129])
            S.dma('pool', self.stsc_d[n][:, ST_ML:ST_ML + 528], sh[:], writes=[('stsc', n, 'ml')])
            self.ml_state_update(t, kin, s, d['dec'][:, s, 4:8], pu=self.P[7 - s % 2])

    def ml_main(self, l, blk):
        S = self.S
        t = self.ml_project(l, blk, True)
        self.premul_gate(t['sg'], 1024)
        qT, kT = self.g16[3], self.g16[4]
        self.transpose_heads(qT, t['q16'])
        self.transpose_heads(kT, t['k16'])
        d = self.decay_prep_all(t['lf'], t['ip'], 4, 'fb')
        kin = self.ml_kin(t, d['wk'][:, :, 0:4])
        hs32 = self.g32[2]
        flat4 = self.g32[4][:].rearrange("p a c -> p (a c)")
        num32s = [flat4[:, 0:528].rearrange("p (h e) -> p h e", h=4), flat4[:, 528:1056].rearrange("p (h e) -> p h e", h=4)]
        tmp32 = flat4[:, 1536:2048]
        Eall = self.g32[3][:].rearrange("p a c -> p (a c)")
        obanks = [self.P[4], self.P[5], self.P[6], self.P[0]]
        pending = None
        for s in range(NSUB):
            n = blk * NSUB + s
            sbl = self.get_sbl(n, 'ml')
            sl = slice(s * 128, (s + 1) * 128)
            shr = self.sh_ml[s % 2][:].rearrange("p (h e) -> p h e", h=4)
            shw = self.sh_ml[(s + 1) % 2][:].rearrange("p (h e) -> p h e", h=4)
            self.ml_state_update(t, kin, s, d['dec'][:, s, 0:4])
            S.cp('act', shw[:, :, 0:129], self.S_ml[:, :, 0:129])
            for h in range(4):
                S.mm(self.P[3][:, h * 128:(h + 1) * 128], kT[:, h, sl], qT[:, h, sl])
            dbank = [self.P[1], self.P[2]]
            for dirn in range(2):
                for h in range(4):
                    c = dirn * 4 + h
                    reg = dbank[dirn][:, h * 128:(h + 1) * 128]
                    S.mm(reg, t['lf'][:, s, c:c + 1].to_broadcast([128, 128]), self.cs('LOW') if dirn == 0 else self.cs('UP'), start=True, stop=False)
                    S.mm(reg, self.identb[:], self.negfb[:] if dirn == 0 else self.negbb[:], start=False, stop=True)
            for dirn in range(2):
                for h in range(4):
                    c = dirn * 4 + h
                    S.act(Eall[:, c * 128:(c + 1) * 128], dbank[dirn][:, h * 128:(h + 1) * 128], AF.Exp, bias=d['bias'][:, s, c:c + 1])
            for dirn in range(2):
                S.tt('dve', self.g16[6][:, 2 * (s % 2) + dirn, :], self.P[3][:, :], Eall[:, dirn * 512:(dirn + 1) * 512], ALU.mult)
            for dirn in range(2):
                pt16 = self.g16[6][:, 2 * (s % 2) + dirn, :]
                num32 = num32s[dirn]
                obanks = [self.P[4], self.P[5], self.P[6], self.P[0]] if dirn == 0 else [self.P[1], self.P[2], self.P[3], self.P[7]]
                for h in range(4):
                    hs = slice(h * 128, (h + 1) * 128)
                    bank = obanks[h]
                    S.mm(bank[:, 0:128], pt16[:, hs], t['v16'][:, s, hs])
                    S.mm(bank[:, 128:129], pt16[:, hs], self.ones16[:, 0:1])
                    S16 = shr[:, h, 0:129] if dirn == 0 else sbl[:, ST_ML + h * 132:ST_ML + h * 132 + 129]
                    S.mm(bank[:, 129:258], qT[:, h, sl], S16)
                for h in range(4):
                    S.cp('act', num32[:, h, 0:129], obanks[h][:, 0:129])
                for h in range(4):
                    c = dirn * 4 + h
                    S.stt('dve', num32[:, h, 0:129], obanks[h][:, 129:258], d['eq'][:, s, c:c + 1], num32[:, h, 0:129], ALU.mult, ALU.add)
                dn = self.tiny[:, 104 + dirn * 4:108 + dirn * 4]
                S.act(dn.unsqueeze(2), num32[:, :, 128:129], AF.Abs)
                S.ts('dve', dn, dn, 1.0, None, ALU.max)
                S.recip(dn, dn)
                h3 = hs32[:, s, :].rearrange("p (h e) -> p h e", h=4)
                if dirn == 0:
                    S.tt('dve', h3, num32[:, :, 0:128], dn.unsqueeze(2).to_broadcast([128, 4, 128]), ALU.mult)
                else:
                    S.tt('dve', tmp32.rearrange("p (h e) -> p h e", h=4), num32[:, :, 0:128], dn.unsqueeze(2).to_broadcast([128, 4, 128]), ALU.mult)
                    S.tt('dve', hs32[:, s, :], hs32[:, s, :], tmp32, ALU.add)
            if pending is not None:
                pending()

            def tail(s=s):
                S.tt('dve', hs32[:, s, :], hs32[:, s, :], t['so'][:, s, :], ALU.mult)
                self.group_norm2(hs32[:, s, :], hs32[:, s, :], 4, 128, True, t['sg'][:, s, :], self.mixed[:, s, 1024:1536], (s % 2) * 48, junk=self.g32[5])
            pending = tail
        pending()

    def hg_lb_setup(self):
        S = self.S
        lg = self.tiny[:, 0:16]
        for l in range(DEPTH):
            for dd in range(2):
                S.dma('pool', lg[:, l * 8 + dd * 4:l * 8 + dd * 4 + 4], self.hg_lb_d[l, dd].rearrange("(t p) -> p t", p=128),
                      allow_slow_non_contiguous=True)
        e = self.tiny[:, 16:32]
        S.act(e, lg, AF.Exp)
        ssum = self.tiny[:, 32:40]
        S.tt('dve', ssum, e[:, 0:8], e[:, 8:16], ALU.add)
        S.recip(ssum, ssum)
        w = self.tiny[:, 40:56]
        S.tt('dve', w[:, 0:8], e[:, 0:8], ssum, ALU.mult)
        S.tt('dve', w[:, 8:16], e[:, 8:16], ssum, ALU.mult)
        lb = self.hglb[:, 0:16]
        S.tt('dve', lb[:, 0:8], w[:, 0:8], w[:, 0:8], ALU.subtract)
        S.tt('dve', lb[:, 8:16], w[:, 0:8], w[:, 8:16], ALU.add)
        S.tt('dve', lb[:, 8:16], lb[:, 8:16], w[:, 0:8], ALU.subtract)
        S.ts('dve', self.hglb[:, 16:32], lb, -1.0, 1.0, ALU.mult, ALU.add)

    def hg_gate_head(self, l, W, dirn, h, main, qT32, qq, kk, kin, slot):
        S = self.S
        if slot == 0:
            A, B, D_, E_ = (self.g32[3][:, i, :] for i in range(4))
            Cc = self.g32[4][:, 0:2, :].rearrange("p a c -> p (a c)")[:, 0:513]
            F_ = self.g32[4][:, 2, :]
        else:
            A, B, D_, E_ = (self.g32[5][:, i, :] for i in range(4))
            Cc = self.g32[2][:, 0:2, :].rearrange("p a c -> p (a c)")[:, 0:513]
            F_ = self.g32[2][:, 2, :]
        kT16 = self.g16[6][:, slot, :]
        col = l * 8 + dirn * 4 + h
        lbc, omc = self.hglb[:, col:col + 1], self.hglb[:, 16 + col:17 + col]
        v3 = lambda ap: ap.rearrange("p (a i) -> p a i", i=64)
        ps = self.proj_fm(W, h)
        one = self.cs('ONES', 0, 1)
        S.act(A, ps, AF.Exp, scale=-1.0)
        yield
        S.act(B, A, AF.Ln, bias=one, scale=lbc)
        S.act(D_, A, AF.Ln, bias=one)
        yield
        S.tt('dve', B, B, D_, ALU.subtract)
        yield
        S.act(A, B, AF.Exp)
        yield
        S.ts('dve', kT16, A, -1.0, 1.0, ALU.mult, ALU.add)
        S.memset('dve', Cc[:, 0:1], 0.0)
        yield
        S.scan(Cc[:, 1:513], self.cs('ONES', 0, 1).to_broadcast([128, 512]), B, 0.0, ALU.mult, ALU.add)
        yield
        if dirn == 0:
            S.tt('dve', v3(D_), v3(Cc[:, 1:513]), Cc[:, 0:512:64].unsqueeze(2).to_broadcast([128, 8, 64]), ALU.subtract)
        else:
            S.tt('dve', v3(D_), Cc[:, 64:513:64].unsqueeze(2).to_broadcast([128, 8, 64]), v3(Cc[:, 1:513]), ALU.subtract)
            S.tt('dve', D_, D_, B, ALU.add)
        yield
        S.act(E_, D_, AF.Exp)
        S.act(F_, D_, AF.Exp, scale=-1.0)
        yield
        dcol = 63 if dirn == 0 else 0
        dsc = self.hgdec[:, (dirn * 4 + h) * 8:(dirn * 4 + h) * 8 + 8]
        S.cp('dve', dsc, E_[:, dcol:512:64])
        if main:
            S.tt('dve', qq[dirn][:, h, :], qT32[:, h, :], E_, ALU.mult)
        yield
        S.tt('pool', kk[dirn][:, h, :], kT16, F_, ALU.mult)
        if (main and dirn == 0) or (not main and dirn == 1):
            S.tt('pool', v3(kin[:, h, :]), v3(kk[dirn][:, h, :]), dsc.unsqueeze(2).to_broadcast([128, 8, 64]), ALU.mult)

    @staticmethod
    def run_interleaved(gens):
        gens = list(gens)
        while gens:
            for g in list(gens):
                try:
                    next(g)
                except StopIteration:
                    gens.remove(g)

    def hg_project(self, l, blk, main):
        S = self.S
        qT32, sg = self.g32[0], self.g32[1]
        qq, kk, kin, v16 = [self.g16[0], self.g16[1]], [self.g16[2], self.g16[3]], self.g16[4], self.g16[5]
        if main:
            W = self.next_piece(l, 4)
            for h in range(4):
                S.cp('act', qT32[:, h, :], self.proj_fm(W, h))
        for dirn in ([0, 1] if main else [1]):
            W = self.next_piece(l, 5 + dirn)
            for hp in (0, 2):
                self.run_interleaved([self.hg_gate_head(l, W, dirn, hp + j, main, qT32, qq, kk, kin, j) for j in range(2)])
        W = self.next_piece(l, 7)
        for s in range(NSUB):
            S.cp('act', v16[:, s, :], self.proj_tm(W, 512, s))
        if main:
            W = self.next_piece(l, 8)
            for s in range(NSUB):
                S.act(sg[:, s, :], self.proj_tm(W, 512, s), AF.Silu)
        return dict(qq=qq, kk=kk, kin=kin, v16=v16, sg=sg)

    def hg_kinT_all(self, t):
        S = self.S
        kinT = self.g16[6]
        views = [self.P[2][:].bitcast(BF16), self.P[3][:].bitcast(BF16)]
        for s in range(NSUB):
            pt = views[s % 2]
            for r in range(2):
                a = 2 * s + r
                for h in range(4):
                    S.tr(pt[r * 64:(r + 1) * 64, h * 128:(h + 1) * 128], t['kin'][:, h, a * 64:(a + 1) * 64], self.identb[:])
            S.cp('act' if s % 2 == 0 else 'dve', kinT[:, s, :], pt[:, 0:512])
        return kinT

    def hg_state_update(self, t, kinT, s, r, dirn, pu=None):
        S = self.S
        a = 2 * s + r
        pr = slice(r * 64, (r + 1) * 64)
        pu = self.P[7] if pu is None else pu
        for h in range(4):
            hs = slice(h * 128, (h + 1) * 128)
            S.mm(pu[:, hs], kinT[pr, s, hs], t['v16'][pr, s, hs])
        for h in range(4):
            c = (dirn * 4 + h) * 8 + a
            S.stt('dve', self.S_hg[:, h, :], self.S_hg[:, h, :], self.hgdec[:, c:c + 1], pu[:, h * 128:(h + 1) * 128], ALU.mult, ALU.add)

    def hg_pre(self, l, blk):
        S = self.S
        t = self.hg_project(l, blk, False)
        kinT = self.hg_kinT_all(t)
        for s in reversed(range(NSUB)):
            n = blk * NSUB + s
            for r in (1, 0):
                a = 2 * s + r
                sh = self.sh_hg[a % 2]
                S.cp('act', sh[:], self.S_hg[:].rearrange("p h e -> p (h e)"))
                S.dma('pool', self.stsc_d[n][:, ST_HG0 + r * 512:ST_HG0 + (r + 1) * 512], sh[:], writes=[('stsc', n, 'hg%d' % r)])
                self.hg_state_update(t, kinT, s, r, 1, pu=self.P[7 - r])

    def hg_main(self, l, blk):
        S = self.S
        t = self.hg_project(l, blk, True)
        self.premul_gate(t['sg'], 512)
        kinT = self.hg_kinT_all(t)
        qq, kk = t['qq'], t['kk']
        o32 = self.g32[2]
        mo, mw = CST['LOW64']
        pending = None
        for s in range(NSUB):
            n = blk * NSUB + s
            sbl = self.get_sbl(n, 'hg')
            po = self.P[4 + s % 2]
            for r in range(2):
                a = 2 * s + r
                asl = slice(a * 64, (a + 1) * 64)
                pr = slice(r * 64, (r + 1) * 64)
                shr = self.sh_hg[a % 2][:].rearrange("p (h e) -> p h e", h=4)
                shw = self.sh_hg[(a + 1) % 2][:].rearrange("p (h e) -> p h e", h=4)
                self.hg_state_update(t, kinT, s, r, 0, pu=self.P[7] if r == 0 else self.P[0])
                S.cp('act', shw, self.S_hg[:])
                for dirn in range(2):
                    for h in range(4):
                        c = dirn * 4 + h
                        S.mm(self.P[3][pr, c * 64:(c + 1) * 64], kk[dirn][:, h, asl], qq[dirn][:, h, asl])
                PTm = self.g32[5][pr, 0, :]
                mask = self.cst[pr, mo:mo + 128].rearrange("p (d i) -> p d i", d=2).unsqueeze(2).to_broadcast([64, 2, 4, 64])
                S.tt('dve', PTm.rearrange("p (d h i) -> p d h i", d=2, h=4), self.P[3][pr, :].rearrange("p (d h i) -> p d h i", d=2, h=4), mask, ALU.mult)
                PT16 = self.g16[4][pr, 0, 0:256]
                S.tt('dve', PT16, PTm[:, 0:256], PTm[:, 256:512], ALU.add)
                for h in range(4):
                    hs = slice(h * 128, (h + 1) * 128)
                    S.mm(po[pr, hs], PT16[:, h * 64:(h + 1) * 64], t['v16'][pr, s, hs], start=True, stop=False)
                    S.mm(po[pr, hs], qq[0][:, h, asl], shr[:, h, :], start=False, stop=False)
                    S.mm(po[pr, hs], qq[1][:, h, asl], sbl[:, ST_HG0 + r * 512 + h * 128:ST_HG0 + r * 512 + (h + 1) * 128], start=False, stop=True)
            if pending is not None:
                pending()
            pending = (lambda po=po, s=s: self.group_norm2(po[:, :], o32[:, s, :], 4, 128, False, t['sg'][:, s, :], self.mixed[:, s, 512:1024], (s % 2) * 48))
        pending()

    def get_sbl(self, n, m):
        if not hasattr(self, '_sbl_held'):
            self._sbl_held = {}
        for nn in (n, n + 1):
            if nn >= self.NCH:
                continue
            if self._sbl_held.get((nn % 2, m)) != (self.cur_layer, nn):
                self._load_sbl(nn, m)
        return self.sbl[n % 2]

    def _load_sbl(self, n, m):
        t = self.sbl[n % 2]
        regs = {'ret': (ST_RET, ST_HG0), 'hg': (ST_HG0, ST_ML), 'ml': (ST_ML, ST_SSD), 'ssd': (ST_SSD, NSTATE)}
        rk = [('stsc', n, 'hg0'), ('stsc', n, 'hg1')] if m == 'hg' else [('stsc', n, m)]
        a, b = regs[m]
        self.S.dma('pool', t[:, a:b], self.stsc_d[n][:, a:b], reads=rk)
        self._sbl_held[(n % 2, m)] = (self.cur_layer, n)

    def pre_block(self, l, blk, src):
        self.cur_layer = l
        self.in_main = False
        self.load_xT(blk, src)
        if 'ret' in self.mixers:
            self.ret_pre(l, blk)
        if 'hg' in self.mixers:
            self.hg_pre(l, blk)
        if 'ml' in self.mixers or 'ssd' in self.mixers:
            self.gates_project(l, blk)
        if 'ml' in self.mixers:
            self.ml_pre(l, blk)
        if 'ssd' in self.mixers:
            self.ssd_pre(l, blk, src)

    def main_block(self, l, blk, src, dst):
        S = self.S
        self.cur_layer = l
        self.in_main = True
        self.load_xT(blk, src)
        if 'ret' in self.mixers:
            self.ret_main(l, blk)
        if 'hg' in self.mixers:
            self.hg_main(l, blk)
        if 'ml' in self.mixers or 'ssd' in self.mixers:
            self.gates_project(l, blk)
        if 'ml' in self.mixers:
            self.ml_main(l, blk)
        if 'ssd' in self.mixers:
            self.ssd_main(l, blk, src)
        if self.debug == 'mixed' or (self.debug == 'mixed1' and l == 1):
            for s in range(NSUB):
                self.dump(self.mixed[:, s, :], blk * NSUB + s, 2048)
            for pid in (17, 18, 19, 20, 21, 22):
                self.next_piece(l, pid)
            return
        self.out_stage(l, blk, src, dst)

    def dump(self, ap, n, width=512):
        S = self.S
        if 'dbg' not in self.dbg_outs:
            self.dbg_outs['dbg'] = self.dout("dbg", [self.T, width])
        for c0 in range(0, width, 512):
            t = self.g32[5][:, 3, :]
            S.cp('dve', t, ap[:, c0:c0 + 512])
            S.dma('pool', self.dbg_outs['dbg'][n * 128:(n + 1) * 128, c0:c0 + 512], t)

    def out_stage(self, l, blk, src, dst):
        S = self.S
        views = [self.P[2][:].bitcast(BF16), self.P[3][:].bitcast(BF16)]
        mT = lambda kt: self.g16[kt // 4][:, kt % 4, :]
        for kt in range(16):
            pt = views[kt % 2]
            for s in range(NSUB):
                S.tr(pt[:, s * 128:(s + 1) * 128], self.mixed[:, s, kt * 128:(kt + 1) * 128], self.identb[:])
            S.cp('act' if kt % 2 == 0 else 'dve', mT(kt), pt[:, 0:512])
        self.load_x32(blk, src, l)
        ident = self.cs('IDENT')
        p32 = self.g32[5][:, :, 0:256]
        S.dma('pool', p32, self.p_d[l][blk * TB:(blk + 1) * TB, :].rearrange("(s p) c -> p s c", p=128))
        pT = self.g16[6][:, 0:2, :]
        for k in range(2):
            bank = self.P[3 + (k % 2)]
            for s in range(NSUB):
                S.tr(bank[:, s * 128:(s + 1) * 128], p32[:, s, k * 128:(k + 1) * 128], ident)
            S.cp('act', pT[:, k, :], bank[:, :])
        zt = lambda s: self.g32[s // 2][:, 2 * (s % 2):2 * (s % 2) + 2, :].rearrange("p a c -> p (a c)")
        for ch in range(2):
            WW = [self.next_piece(l, 17 + 2 * ch), self.next_piece(l, 18 + 2 * ch)]
            for s in range(NSUB):
                bank = [self.P[0], self.P[1], self.P[7]][self.pj3_i % 3]
                self.pj3_i += 1
                for kt in range(16):
                    S.mm(bank[:, :], mT(kt)[:, s * 128:(s + 1) * 128], WW[kt // 8][:, kt % 8, :], start=(kt == 0), stop=(kt == 15))
                S.stt('dve', zt(s)[:, ch * 512:(ch + 1) * 512], self.xs32(s)[:, ch * 512:(ch + 1) * 512], float(DN_ALPHA), bank[:, :], ALU.mult, ALU.add)
        tn = self.tiny
        xnT = (self.g16[4], self.g16[5])
        if blk + 1 < self.NBLK:
            self.load_x32(blk + 1, src, l)
            self.x_prefetched = (l, blk + 1, True)
        st = lambda i: tn[:, 96 + 4 * i:100 + 4 * i]
        s1, ss, mean, var, rstd, nmr = (st(i) for i in range(6))
        epsc = self.cs('COLS', 4, 5)
        for s in range(NSUB):
            junk = self.g32[4 + s // 2][:, 2 * (s % 2):2 * (s % 2) + 2, :].rearrange("p a c -> p (a c)")
            S.act(junk, zt(s), AF.Square, accum=ss[:, s:s + 1])
        for s in range(NSUB):
            S.rsum('dve', s1[:, s:s + 1], zt(s))
        S.ts('dve', mean, s1, 1.0 / DM, None, ALU.mult)
        S.tt('dve', var, mean, mean, ALU.mult)
        S.stt('dve', var, ss, 1.0 / DM, var, ALU.mult, ALU.subtract)
        S.act(rstd, var, AF.Ln, bias=epsc)
        S.act(rstd, rstd, AF.Exp, scale=-0.5)
        S.stt('dve', nmr, mean, -1.0, rstd, ALU.mult, ALU.mult)
        for s in range(NSUB):
            S.act(zt(s), zt(s), AF.Identity, bias=nmr[:, s:s + 1], scale=rstd[:, s:s + 1])
        for s in range(NSUB):
            S.tt('pool' if s % 2 else 'dve', zt(s), zt(s), self.ln_g[:], ALU.mult)
        for s in range(NSUB):
            S.tt('dve', zt(s), zt(s), self.ln_b[:], ALU.add)
        for k in range(8):
            bank = self.P[3 + (k % 2)]
            for s in range(NSUB):
                S.tr(bank[:, s * 128:(s + 1) * 128], zt(s)[:, k * 128:(k + 1) * 128], ident)
            S.cp('act' if k % 2 == 0 else 'dve', xnT[k // 4][:, k % 4, :], bank[:, :])
        o32 = self.g32[4], self.g32[5]
        ot = lambda s: o32[s // 2][:, 2 * (s % 2):2 * (s % 2) + 2, :].rearrange("p a c -> p (a c)")
        for ch in range(2):
            Wg = self.next_piece(l, 21 + ch)
            for s in range(NSUB):
                bank = [self.P[0], self.P[1], self.P[7]][self.pj3_i % 3]
                self.pj3_i += 1
                self.pj_i += 1
                for k in range(8):
                    S.mm(bank[:, :], xnT[k // 4][:, k % 4, s * 128:(s + 1) * 128], Wg[:, k, :], start=(k == 0), stop=(k == 7))
                gt = ot(s)[:, ch * 512:(ch + 1) * 512]
                S.act(gt, bank[:, :], AF.Sigmoid)
                bank2 = self.P[5 + (self.pj_i % 2)]
                for k in range(2):
                    S.mm(bank2[:, :], pT[:, k, s * 128:(s + 1) * 128], self.wpp[:, k, ch * 512:(ch + 1) * 512], start=(k == 0), stop=(k == 1))
                S.tt('dve', gt, gt, bank2[:, :], ALU.mult)
                S.tt('pool', gt, gt, zt(s)[:, ch * 512:(ch + 1) * 512], ALU.add)
        for s in range(NSUB):
            S.dma('pool', dst[blk * TB + s * 128:blk * TB + (s + 1) * 128, :], ot(s),
                  writes=[('hx', l + 1, blk * NSUB + s)])


def build_program(T, **kw):
    pr = Prog(T, **kw)
    nc = pr.build()
    return nc, pr


_CACHE = {}


def make_in_map(x, p, prm, cst, rope):
    f = lambda a: np.ascontiguousarray(a, dtype=np.float32)
    return {
        "x": f(x), "pp": f(p),
        "w_in": f(prm['w_in']), "w_out": f(prm['w_out']),
        "w_ple_gate": f(prm['w_ple_gate']), "w_ple_proj": f(prm['w_ple_proj']),
        "ln_g": f(prm['ln_g']), "ln_b": f(prm['ln_b']),
        "ret_log_rate": f(prm['ret_log_rate']).reshape(DEPTH, 8),
        "gn_g": f(np.concatenate([prm['ret_norm_g'], prm['hgrn_norm_g'], prm['mlstm_norm_g'], prm['ssd_norm_g']], axis=1)),
        "hgrn_lb_logits": f(prm['hgrn_lb_logits']),
        "ml_bias": f(np.concatenate([np.reshape(prm['mlstm_i_bias'], (DEPTH, 8)), np.reshape(prm['mlstm_f_bias'], (DEPTH, 8))], axis=1)),
        "ssd_conv_w": f(prm['ssd_conv_w']), "ssd_conv_b": f(prm['ssd_conv_b']),
        "ssd_a_log": f(prm['ssd_a_log']).reshape(DEPTH, 16), "ssd_dt_bias": f(prm['ssd_dt_bias']).reshape(DEPTH, 16),
        "ssd_d": f(prm['ssd_d']),
        "cst": cst, "rope": rope,
    }


def kernel(x_prompt, x_sample, p_prompt, p_sample, **prm):
    x_prompt = np.asarray(x_prompt)
    x_sample = np.asarray(x_sample)
    p_prompt = np.asarray(p_prompt)
    p_sample = np.asarray(p_sample)
    prm = {k: np.asarray(v) for k, v in prm.items()}
    T = x_prompt.shape[1]
    nb, ns = x_prompt.shape[0], x_sample.shape[0]
    seqs = [(x_prompt[i], p_prompt[:, i]) for i in range(nb)] + [(x_sample[i], p_sample[:, i]) for i in range(ns)]
    ncores = 8
    if T not in _CACHE:
        _CACHE[T] = build_program(T)[0]
    nc = _CACHE[T]
    cst, rope = host_consts(T)
    in_maps = []
    for c in range(ncores):
        xs, ps_ = seqs[c % len(seqs)]
        in_maps.append(make_in_map(xs, ps_, prm, cst, rope))
    res = run_bass_kernel_spmd(nc, in_maps, core_ids=list(range(ncores)))
    outs = [np.asarray(res.results[c]["y"], dtype=np.float32) for c in range(len(seqs))]
    y_prompt = np.stack(outs[:nb], axis=0)
    y_sample = np.stack(outs[nb:], axis=0)
    return (y_prompt, y_sample)
```
